# Optimizing a Trainium2 kernel written in Bass

```python
import math
import jax, jax.numpy as jnp
from jax import lax
import numpy as np

D_MODEL = 2048
BATCH = 4
SEQ = 4096
DEPTH = 1

D_MIX = D_MODEL
RET_HEADS = 4
RET_DK = 256
RET_DV = 256
RET_CHUNK = 128
ROPE_BASE = 10000.0
NSA_HEADS = 8
NSA_KV_GROUPS = 2
NSA_HPG = NSA_HEADS // NSA_KV_GROUPS
NSA_DH = 128
CMP_LEN = 32
CMP_STRIDE = 16
SEL_LEN = 64
SEL_TOPK = 16
SEL_Q_BLOCK = 64
WIN = 512
WIN_Q_BLOCK = 128
N_BRANCH = 3
REL_BUCKETS = 32
REL_MAX_DIST = 128
D_FF = 5632
EPS = 1e-6
NEG = -1e30
FORCE = 1e4

RET_W = RET_HEADS * RET_DV
NSA_W = NSA_HEADS * NSA_DH
KV_W = NSA_KV_GROUPS * NSA_DH
IN_SPLITS = [RET_HEADS * RET_DK, RET_HEADS * RET_DK, RET_W, RET_W, NSA_W,
             KV_W, KV_W, KV_W, KV_W, KV_W, KV_W, NSA_HEADS * N_BRANCH]
D_IN = sum(IN_SPLITS)

kernel_name = "hymba_retnet_nsa_macaron_block"


def rmsnorm(x, g):
    xf = x.astype(jnp.float32)
    y = xf * lax.rsqrt(jnp.mean(xf * xf, axis=-1, keepdims=True) + EPS)
    return (y * g.astype(jnp.float32)).astype(x.dtype)


def swiglu(x, w1, w3, w2):
    return (jax.nn.silu(x @ w1) * (x @ w3)) @ w2


def t5_bucket(rel):
    n = jnp.maximum(rel, 0)
    max_exact = REL_BUCKETS // 2
    nf = jnp.maximum(n, 1).astype(jnp.float32)
    large = max_exact + (jnp.log(nf / max_exact) / math.log(REL_MAX_DIST / max_exact)
                         * (REL_BUCKETS - max_exact)).astype(jnp.int32)
    large = jnp.minimum(large, REL_BUCKETS - 1)
    return jnp.where(n < max_exact, n, large)


def masked_softmax(logits, mask):
    p = jax.nn.softmax(jnp.where(mask, logits, NEG), axis=-1)
    return jnp.where(mask, p, 0.0)


def rotary(x, pos):
    half = x.shape[-1] // 2
    inv = ROPE_BASE ** (-jnp.arange(half, dtype=jnp.float32) / half)
    ang = pos.astype(jnp.float32)[:, None] * inv[None, :]
    cos = jnp.cos(ang)[None, :, None, :]
    sin = jnp.sin(ang)[None, :, None, :]
    xf = x.astype(jnp.float32)
    x1, x2 = xf[..., :half], xf[..., half:]
    return jnp.concatenate([x1 * cos - x2 * sin, x1 * sin + x2 * cos], axis=-1)


def retention(q, k, v):
    B, S, H, DK = q.shape
    DV = v.shape[-1]
    C = RET_CHUNK
    N = S // C
    log_g = jnp.log(1.0 - 2.0 ** (-5.0 - jnp.arange(H, dtype=jnp.float32)))
    idx = jnp.arange(C, dtype=jnp.float32)
    diff = idx[:, None] - idx[None, :]
    inner_decay = jnp.where(diff >= 0.0,
                            jnp.exp(jnp.maximum(diff, 0.0)[None] * log_g[:, None, None]), 0.0)
    xi = jnp.exp((idx + 1.0)[None] * log_g[:, None])
    zeta = jnp.exp((C - 1.0 - idx)[None] * log_g[:, None])
    chunk_decay = jnp.exp(C * log_g)
    qc = q.reshape(B, N, C, H, DK)
    kc = k.reshape(B, N, C, H, DK)
    vc = v.reshape(B, N, C, H, DV)
    s = jnp.einsum('bnchd,bnmhd->bnhcm', qc, kc) * inner_decay
    y_in = jnp.einsum('bnhcm,bnmhe->bnche', s, vc)
    u = jnp.einsum('bnmhd,bnmhe,hm->nbhde', kc, vc, zeta)

    def step(state, u_n):
        return chunk_decay[None, :, None, None] * state + u_n, state

    _, r_prev = lax.scan(step, jnp.zeros((B, H, DK, DV), jnp.float32), u)
    y_cross = jnp.einsum('bnchd,nbhde,hc->bnche', qc, r_prev, xi)
    return (y_in + y_cross).reshape(B, S, H, DV)


def compress(kv, pe, w1, w2, cmp_idx):
    blk = kv[:, :, cmp_idx] + pe.astype(kv.dtype)
    hdn = jax.nn.silu(jnp.einsum('bgnld,ldf->bgnf', blk, w1))
    return hdn @ w2


def nsa(q, kc_, vc_, ks_, vs_, kw_, vw_, gates, pe_k, w1_k, w2_k, pe_v, w1_v, w2_v, rel_bias):
    B, S, H, dh = q.shape
    G, hpg = NSA_KV_GROUPS, NSA_HPG
    scale = dh ** -0.5
    pos = jnp.arange(S, dtype=jnp.int32)
    qg = q.reshape(B, S, G, hpg, dh).transpose(0, 2, 3, 1, 4)
    to_g = lambda t: t.reshape(B, S, G, dh).transpose(0, 2, 1, 3)
    kc_, vc_, ks_, vs_, kw_, vw_ = map(to_g, (kc_, vc_, ks_, vs_, kw_, vw_))

    n_cmp = (S - CMP_LEN) // CMP_STRIDE + 1
    cmp_idx = np.arange(n_cmp)[:, None] * CMP_STRIDE + np.arange(CMP_LEN)[None, :]
    k_cmp = compress(kc_, pe_k, w1_k, w2_k, cmp_idx)
    v_cmp = compress(vc_, pe_v, w1_v, w2_v, cmp_idx)
    cmp_end = jnp.asarray(np.arange(n_cmp) * CMP_STRIDE + CMP_LEN - 1, jnp.int32)
    rel_c = pos[:, None] - cmp_end[None, :]
    bias_c = rel_bias.astype(jnp.float32)[:, t5_bucket(rel_c)].reshape(G, hpg, S, n_cmp)
    logit_c = jnp.einsum('bghsd,bgnd->bghsn', qg, k_cmp).astype(jnp.float32) * scale + bias_c
    p_cmp = masked_softmax(logit_c, rel_c >= 0)
    o_cmp = jnp.einsum('bghsn,bgnd->bghsd', p_cmp.astype(v_cmp.dtype), v_cmp)

    n_sel = S // SEL_LEN
    n_top = min(SEL_TOPK, n_sel)
    sel_of = cmp_idx // SEL_LEN
    overlap = jnp.asarray((sel_of[:, :, None] == np.arange(n_sel)[None, None, :]).sum(1)
                          .astype(np.float32) / CMP_LEN)
    imp = jnp.einsum('bghsn,nj->bgsj', p_cmp, overlap)
    blk = jnp.arange(n_sel, dtype=jnp.int32)
    cur = pos // SEL_LEN
    causal = (blk[None, :] * SEL_LEN) <= pos[:, None]
    forced = (blk[None, :] == 0) | (blk[None, :] == cur[:, None]) | (blk[None, :] == cur[:, None] - 1)
    score = jnp.where(forced, FORCE, jnp.where(causal, imp, NEG))
    top_val, top_idx = lax.top_k(score, n_top)
    top_ok = top_val > (NEG * 0.5)

    k_blocks = ks_.reshape(B, G, n_sel, SEL_LEN, dh)
    v_blocks = vs_.reshape(B, G, n_sel, SEL_LEN, dh)
    bi = jnp.arange(B)[:, None, None, None]
    gi = jnp.arange(G)[None, :, None, None]
    tbl_g = rel_bias.astype(jnp.float32).reshape(G, hpg, REL_BUCKETS).transpose(0, 2, 1)
    Qb = SEL_Q_BLOCK
    nq = S // Qb

    def sel_block(args):
        qb, idxb, okb, posb = args
        kg = k_blocks[bi, gi, idxb].reshape(B, G, Qb, n_top * SEL_LEN, dh)
        vg = v_blocks[bi, gi, idxb].reshape(B, G, Qb, n_top * SEL_LEN, dh)
        kpos = (idxb[..., None] * SEL_LEN + jnp.arange(SEL_LEN, dtype=jnp.int32)).reshape(B, G, Qb, -1)
        rel = posb[None, None, :, None] - kpos
        mask = jnp.repeat(okb, SEL_LEN, axis=-1) & (rel >= 0)
        bias = jnp.moveaxis(tbl_g[gi, t5_bucket(rel)], -1, 2)
        logits = jnp.einsum('bghqd,bgqkd->bghqk', qb, kg).astype(jnp.float32) * scale + bias
        p = masked_softmax(logits, mask[:, :, None])
        return jnp.einsum('bghqk,bgqkd->bghqd', p.astype(vg.dtype), vg)

    q_blocks = jnp.moveaxis(qg.reshape(B, G, hpg, nq, Qb, dh), 3, 0)
    idx_blocks = jnp.moveaxis(top_idx.reshape(B, G, nq, Qb, n_top), 2, 0)
    ok_blocks = jnp.moveaxis(top_ok.reshape(B, G, nq, Qb, n_top), 2, 0)
    pos_blocks = pos.reshape(nq, Qb)
    o_sel = lax.map(sel_block, (q_blocks, idx_blocks, ok_blocks, pos_blocks))
    o_sel = jnp.moveaxis(o_sel, 0, 3).reshape(B, G, hpg, S, dh)

    Wb = WIN_Q_BLOCK
    nb = S // Wb
    nw = WIN // Wb
    Kw = (nw + 1) * Wb

    def band(kv):
        padded = jnp.pad(kv, ((0, 0), (0, 0), (WIN, 0), (0, 0))).reshape(B, G, nb + nw, Wb, dh)
        return jnp.concatenate([padded[:, :, i:i + nb] for i in range(nw + 1)], axis=3)

    kwb, vwb = band(kw_), band(vw_)
    qi = np.arange(Wb)[:, None]
    kj = np.arange(Kw)[None, :]
    rel_w = qi + WIN - kj
    in_band = (rel_w >= 0) & (rel_w < WIN)
    real = (np.arange(nb)[:, None] * Wb + np.arange(Kw)[None, :] - WIN) >= 0
    mask_w = jnp.asarray(in_band[None] & real[:, None, :])
    bias_w = rel_bias.astype(jnp.float32)[:, t5_bucket(jnp.asarray(rel_w, jnp.int32))]
    bias_w = bias_w.reshape(G, hpg, 1, Wb, Kw)
    qw = qg.reshape(B, G, hpg, nb, Wb, dh)
    logit_w = jnp.einsum('bghnqd,bgnkd->bghnqk', qw, kwb).astype(jnp.float32) * scale + bias_w
    p_w = masked_softmax(logit_w, mask_w)
    o_win = jnp.einsum('bghnqk,bgnkd->bghnqd', p_w.astype(vwb.dtype), vwb).reshape(B, G, hpg, S, dh)

    g = jax.nn.sigmoid(gates.astype(jnp.float32)).reshape(B, S, G, hpg, N_BRANCH).transpose(0, 2, 3, 1, 4)
    o = (g[..., 0:1] * o_cmp.astype(jnp.float32) + g[..., 1:2] * o_sel.astype(jnp.float32)
         + g[..., 2:3] * o_win.astype(jnp.float32))
    return o.transpose(0, 3, 1, 2, 4).reshape(B, S, H * dh).astype(q.dtype)


def token_mix(h, w_in, ret_gn_gain, pe_k, w1_k, w2_k, pe_v, w1_v, w2_v, w_out, rel_bias):
    B, S, _ = h.shape
    proj = h @ w_in
    cols = []
    off = 0
    for w in IN_SPLITS:
        cols.append(proj[..., off:off + w])
        off += w
    rq, rk, rv, rg, nq_, kc_, vc_, ks_, vs_, kw_, vw_, ngate = cols
    pos = jnp.arange(S, dtype=jnp.int32)
    q_r = rotary(rq.reshape(B, S, RET_HEADS, RET_DK), pos)
    k_r = rotary(rk.reshape(B, S, RET_HEADS, RET_DK), pos) * (RET_DK ** -0.5)
    v_r = rv.reshape(B, S, RET_HEADS, RET_DV).astype(jnp.float32)
    y = retention(q_r, k_r, v_r)
    mu = jnp.mean(y, axis=-1, keepdims=True)
    var = jnp.mean((y - mu) ** 2, axis=-1, keepdims=True)
    y = ((y - mu) * lax.rsqrt(var + EPS)).reshape(B, S, RET_W) * ret_gn_gain.astype(jnp.float32)
    y_ret = (jax.nn.silu(rg.astype(jnp.float32)) * y).astype(h.dtype)
    y_nsa = nsa(nq_.reshape(B, S, NSA_HEADS, NSA_DH), kc_, vc_, ks_, vs_, kw_, vw_, ngate,
                pe_k, w1_k, w2_k, pe_v, w1_v, w2_v, rel_bias)
    return jnp.concatenate([y_ret, y_nsa], axis=-1) @ w_out


def setup_inputs(seed: int = 0) -> dict:
    key = jax.random.key(seed)
    ks = jax.random.split(key, 24)
    f32 = jnp.float32

    def nrm(k, shape, scale):
        return jax.random.normal(k, shape, f32) * scale

    def gain(k, shape):
        return 1.0 + 0.05 * jax.random.normal(k, shape, f32)

    L, dh = CMP_LEN, NSA_DH
    return {
        "x": nrm(ks[0], (BATCH, SEQ, D_MODEL), 1.0),
        "ffn1_norm": gain(ks[1], (DEPTH, D_MODEL)),
        "ffn1_w1": nrm(ks[2], (DEPTH, D_MODEL, D_FF), D_MODEL ** -0.5),
        "ffn1_w3": nrm(ks[3], (DEPTH, D_MODEL, D_FF), D_MODEL ** -0.5),
        "ffn1_w2": nrm(ks[4], (DEPTH, D_FF, D_MODEL), D_FF ** -0.5),
        "mix_norm": gain(ks[5], (DEPTH, D_MODEL)),
        "w_in": nrm(ks[6], (DEPTH, D_MODEL, D_IN), D_MODEL ** -0.5),
        "ret_gn_gain": gain(ks[7], (DEPTH, RET_W)),
        "cmp_pe_k": nrm(ks[8], (DEPTH, L, dh), 0.1),
        "cmp_w1_k": nrm(ks[9], (DEPTH, L, dh, dh), (L * dh) ** -0.5),
        "cmp_w2_k": nrm(ks[10], (DEPTH, dh, dh), dh ** -0.5),
        "cmp_pe_v": nrm(ks[11], (DEPTH, L, dh), 0.1),
        "cmp_w1_v": nrm(ks[12], (DEPTH, L, dh, dh), (L * dh) ** -0.5),
        "cmp_w2_v": nrm(ks[13], (DEPTH, dh, dh), dh ** -0.5),
        "w_out": nrm(ks[14], (DEPTH, D_MIX, D_MODEL), D_MIX ** -0.5),
        "ffn2_norm": gain(ks[15], (DEPTH, D_MODEL)),
        "ffn2_w1": nrm(ks[16], (DEPTH, D_MODEL, D_FF), D_MODEL ** -0.5),
        "ffn2_w3": nrm(ks[17], (DEPTH, D_MODEL, D_FF), D_MODEL ** -0.5),
        "ffn2_w2": nrm(ks[18], (DEPTH, D_FF, D_MODEL), D_FF ** -0.5),
        "rel_bias": nrm(ks[19], (NSA_HEADS, REL_BUCKETS), 0.5),
        "final_norm": gain(ks[20], (D_MODEL,)),
    }


def reference(x, ffn1_norm, ffn1_w1, ffn1_w3, ffn1_w2, mix_norm, w_in, ret_gn_gain,
              cmp_pe_k, cmp_w1_k, cmp_w2_k, cmp_pe_v, cmp_w1_v, cmp_w2_v, w_out,
              ffn2_norm, ffn2_w1, ffn2_w3, ffn2_w2, rel_bias, final_norm):
    for l in range(DEPTH):
        x = x + 0.5 * swiglu(rmsnorm(x, ffn1_norm[l]), ffn1_w1[l], ffn1_w3[l], ffn1_w2[l])
        x = x + token_mix(rmsnorm(x, mix_norm[l]), w_in[l], ret_gn_gain[l],
                          cmp_pe_k[l], cmp_w1_k[l], cmp_w2_k[l],
                          cmp_pe_v[l], cmp_w1_v[l], cmp_w2_v[l], w_out[l], rel_bias)
        x = x + 0.5 * swiglu(rmsnorm(x, ffn2_norm[l]), ffn2_w1[l], ffn2_w3[l], ffn2_w2[l])
    return rmsnorm(x, final_norm)
```

```python
import contextlib
import math
import numpy as np
import concourse.bass as bass
import concourse.mybir as mybir
from concourse.bass_utils import run_bass_kernel_spmd

F32 = mybir.dt.float32
BF16 = mybir.dt.bfloat16
ALU = mybir.AluOpType
AF = mybir.ActivationFunctionType
AX = mybir.AxisListType

D = 2048
DFF = 5632
NTOK = 2048
SEQ = 4096
EPS = 1e-6
NCH = D // 128
NFF = DFF // 128


def freeze(fn):
    import types
    if getattr(fn, "__closure__", None) is None:
        return fn
    cells = []
    for c in fn.__closure__:
        try:
            cells.append(types.CellType(c.cell_contents))
        except ValueError:
            cells.append(c)
    return types.FunctionType(fn.__code__, fn.__globals__, fn.__name__, fn.__defaults__, tuple(cells))


_SEM_REG = {}


class Prog:
    ENG = ("pe", "act", "dve", "pool", "sp")

    def __init__(self, nc, es, tag):
        self.nc = nc
        self.es = es
        self.tag = tag
        self.q = {e: [] for e in self.ENG}
        reg = _SEM_REG.setdefault(id(nc), {"es": contextlib.ExitStack(), "sems": {}, "cnt": {}, "nc": nc})
        self.reg = reg
        self.sems = reg["sems"]
        self.cnt = reg["cnt"]
        self.seen = {e: {} for e in self.ENG}
        for e in self.ENG:
            self.newsem("p_" + e)
        self.nsem = 0

    def newsem(self, name):
        if name not in self.sems:
            h = self.reg["es"].enter_context(self.nc.semaphore("g_" + name))
            self.sems[name] = h
            self.cnt[name] = 0
        return name

    def slot_sem(self):
        self.nsem += 1
        return self.newsem("s%d" % self.nsem)

    def _waits(self, eng, waits):
        for tok in waits:
            if tok is None:
                continue
            if isinstance(tok, list):
                self._waits(eng, tok)
                continue
            name, val = tok
            if self.seen[eng].get(name, 0) >= val:
                continue
            self.seen[eng][name] = val
            self.q[eng].append(("w", name, val))

    def op(self, eng, fn, waits=(), sig=True):
        self._waits(eng, waits)
        fn = freeze(fn)
        if sig:
            name = "p_" + eng
            self.cnt[name] += 1
            self.q[eng].append(("i", fn, name, 1))
            return (name, self.cnt[name])
        self.q[eng].append(("i", fn, None, 0))
        return None

    def dma(self, eng, out, in_, sem, waits=(), **kw):
        self._waits(eng, waits)
        self.cnt[sem] += 16
        self.q[eng].append(("i", lambda e: e.dma_start(out=out, in_=in_, **kw), sem, 16))
        return (sem, self.cnt[sem])

    def raw(self, eng, fn, waits=()):
        self._waits(eng, waits)
        self.q[eng].append(("r", freeze(fn)))

    def emit(self):
        nc = self.nc

        def replay(eobj, items):
            for it in items:
                if it[0] == "w":
                    eobj.wait_ge(self.sems[it[1]], it[2])
                elif it[0] == "r":
                    it[1](eobj)
                elif it[0] == "c":
                    it[1](eobj).then_inc(self.sems[it[2]])
                else:
                    ins = it[1](eobj)
                    if it[2] is not None:
                        ins.then_inc(self.sems[it[2]], it[3])

        with nc.Block() as block:
            block.tensor(lambda e: replay(e, self.q["pe"]))
            block.scalar(lambda e: replay(e, self.q["act"]))
            block.vector(lambda e: replay(e, self.q["dve"]))
            block.gpsimd(lambda e: replay(e, self.q["pool"]))
            block.sync(lambda e: replay(e, self.q["sp"]))


class Ring:
    def __init__(self, bufs):
        self.bufs = bufs
        self.n = len(bufs)
        self.i = 0
        self.free = [None] * self.n

    def next(self):
        k = self.i % self.n
        self.i += 1
        return k, self.bufs[k], self.free[k]

    def release(self, k, tok):
        self.free[k] = tok


def sb(nc, es, name, shape, dt):
    return es.enter_context(nc.sbuf_tensor(name, shape, dt))


def ps(nc, es, name, shape, dt=F32):
    return es.enter_context(nc.psum_tensor(name, shape, dt))


def bcast_rows(ap1d_tensor, offset, n):
    return bass.AP(tensor=ap1d_tensor, offset=offset, ap=[[0, 128], [1, n]])


class NormCtx:
    def __init__(self, nc, es, P, tag, gain_ap, ident_ap, psum_banks):
        self.P = P
        self.gain_bc = sb(nc, es, tag + "gain", [128, D], F32)
        self.ident = sb(nc, es, tag + "ident", [128, 128], BF16)
        self.ss = sb(nc, es, tag + "ss", [128, 64], F32)
        self.rstd = sb(nc, es, tag + "rstd", [128, 64], F32)
        self.xn = Ring([sb(nc, es, tag + "xn%d" % i, [128, D], BF16) for i in range(2)])
        self.banks = psum_banks
        s = P.slot_sem()
        P.dma("sp", self.gain_bc[:, :], gain_ap, s)
        self.t_const = P.dma("sp", self.ident[:, :], ident_ap, s)
        self.col = 0

    def tile(self, xt, xt_tok, dst_fn, want_rows=None):
        P = self.P
        col = self.col % 64
        self.col += 1
        ss, rstd, gain_bc, ident = self.ss, self.rstd, self.gain_bc, self.ident
        assert self.col <= 64
        k, xn, fr = self.xn.next()
        t_sq = P.op("act", lambda e: e.activation(out=xn[:, :], in_=xt[:, :], func=AF.Square,
                                                  accum_out=ss[:, col:col + 1]),
                    waits=[xt_tok, fr])
        t_a = P.op("dve", lambda e: e.tensor_scalar(out=rstd[:, col:col + 1], in0=ss[:, col:col + 1],
                                                    scalar1=1.0 / D, scalar2=EPS, op0=ALU.mult, op1=ALU.add),
                   waits=[t_sq])
        t_a2 = P.op("act", lambda e: e.sqrt(out=ss[:, col:col + 1], in_=rstd[:, col:col + 1]), waits=[t_a])
        t_b = P.op("dve", lambda e: e.reciprocal(out=rstd[:, col:col + 1], in_=ss[:, col:col + 1]), waits=[t_a2])
        if want_rows is not None:
            t_n = want_rows(rstd[:, col:col + 1], [t_b, self.t_const])
            return [t_n], t_n
        t_n = P.op("dve", lambda e: e.scalar_tensor_tensor(out=xn[:, :], in0=xt[:, :], scalar=rstd[:, col:col + 1],
                                                           in1=gain_bc[:, :], op0=ALU.mult, op1=ALU.mult),
                   waits=[t_b, fr, self.t_const])
        toks = []
        tk = None
        for half in range(2):
            kp, pT, frp = self.banks.next()
            pTb = pT[:, :].bitcast(BF16)
            for c8 in range(8):
                c = half * 8 + c8
                tk = P.op("pe", lambda e, c=c, c8=c8, pTb=pTb, xn=xn: e.transpose(
                    out=pTb[:, c8 * 128:(c8 + 1) * 128], in_=xn[:, c * 128:(c + 1) * 128], identity=ident[:, :]),
                    waits=[t_n, frp] if c8 == 0 else (), sig=(c8 == 7))
            tc = dst_fn(half, pTb, tk)
            self.banks.release(kp, tc)
            toks.append(tc)
        self.xn.release(k, tk)
        return toks, t_n


def ffn_phase(nc, tag, C, x_src, gain_t, w1_d, w3_d, w2_d, x_dst):
    TG = 1024
    NTT = TG // 128
    WB = 256
    NBLK = DFF // WB
    aT_free_holder = [None]
    with contextlib.ExitStack() as es:
        P = Prog(nc, es, tag)
        hT = sb(nc, es, tag + "hT", [128, NCH, TG], BF16)
        aT = sb(nc, es, tag + "aT", [128, NFF, TG], BF16)
        xin = Ring([sb(nc, es, tag + "xin%d" % i, [128, D], F32) for i in range(2)])
        s_xin = [P.slot_sem() for _ in range(2)]
        wa = Ring([sb(nc, es, tag + "wa%d" % i, [128, NCH, WB], BF16) for i in range(2)])
        wb = Ring([sb(nc, es, tag + "wb%d" % i, [128, NCH, WB], BF16) for i in range(2)])
        s_wa = [P.slot_sem() for _ in range(2)]
        s_wb = [P.slot_sem() for _ in range(2)]
        silu = Ring([sb(nc, es, tag + "silu%d" % i, [128, 512], F32) for i in range(2)])
        w2r = Ring([sb(nc, es, tag + "w2r%d" % i, [128, 4, 512], BF16) for i in range(2)])
        s_w2 = [P.slot_sem() for _ in range(2)]
        xres = Ring([sb(nc, es, tag + "xres%d" % i, [128, 512], F32) for i in range(2)])
        s_xr = [P.slot_sem() for _ in range(2)]
        xo = Ring([sb(nc, es, tag + "xo%d" % i, [128, 512], F32) for i in range(2)])
        s_xo = [P.slot_sem() for _ in range(2)]
        banks = Ring([ps(nc, es, tag + "ps%d" % i, [128, 512]) for i in range(8)])
        NC_ = NormCtx(nc, es, P, tag, bcast_rows(gain_t, 0, D), C["ident"], banks)

        w1v = w1_d.rearrange("(c p) n -> p c n", p=128)
        w3v = w3_d.rearrange("(c p) n -> p c n", p=128)

        for grp in range(NTOK // TG):
            tok0 = grp * TG
            hT_ready = []

            def load_x(tt):
                k, buf, fr = xin.next()
                t = P.dma("sp", buf[:, :], x_src[tok0 + tt * 128: tok0 + (tt + 1) * 128, :], s_xin[k], waits=[fr])
                return k, buf, t

            nxt = load_x(0)
            for tt in range(NTT):
                k, buf, t = nxt
                if tt + 1 < NTT:
                    nxt = load_x(tt + 1)

                def dst(half, pTb, tk, tt=tt):
                    return P.op("act", lambda e: e.copy(
                        out=hT[:, half * 8:(half + 1) * 8, tt * 128:(tt + 1) * 128],
                        in_=pTb.rearrange("p (c t) -> p c t", c=8)), waits=[tk, aT_free_holder[0]])
                toks, t_free = NC_.tile(buf, t, dst)
                xin.release(k, t_free)
                hT_ready += toks
            def load_w(blk):
                ka, a, fa = wa.next()
                ta = P.dma("pool", a[:, :, :], w1v[:, :, blk * WB:(blk + 1) * WB], s_wa[ka], waits=[fa])
                kb, b, fb = wb.next()
                tb = P.dma("pool", b[:, :, :], w3v[:, :, blk * WB:(blk + 1) * WB], s_wb[kb], waits=[fb])
                return (ka, a, ta, kb, b, tb)

            nw = load_w(0)
            aT_ready = []
            for blk in range(NBLK):
                ka, a, ta, kb, b, tb = nw
                if blk + 1 < NBLK:
                    nw = load_w(blk + 1)
                last_pe = None
                for sub in range(WB // 128):
                    j = blk * (WB // 128) + sub
                    for th in range(TG // 512):
                        kA, bA, fA = banks.next()
                        kB, bB, fB = banks.next()
                        for c in range(NCH):
                            tA = P.op("pe", lambda e, c=c, a=a, bA=bA, sub=sub, th=th: e.matmul(
                                bA[:, :], lhsT=a[:, c, sub * 128:(sub + 1) * 128], rhs=hT[:, c, th * 512:(th + 1) * 512],
                                start=(c == 0), stop=(c == NCH - 1)),
                                waits=[ta, fA, hT_ready] if c == 0 else (), sig=(c == NCH - 1))
                        for c in range(NCH):
                            tB = P.op("pe", lambda e, c=c, b=b, bB=bB, sub=sub, th=th: e.matmul(
                                bB[:, :], lhsT=b[:, c, sub * 128:(sub + 1) * 128], rhs=hT[:, c, th * 512:(th + 1) * 512],
                                start=(c == 0), stop=(c == NCH - 1)),
                                waits=[tb, fB] if c == 0 else (), sig=(c == NCH - 1))
                        last_pe = tB
                        ks, sbuf_, fs = silu.next()
                        t_s = P.op("act", lambda e, bA=bA, sbuf_=sbuf_: e.activation(
                            out=sbuf_[:, :], in_=bA[:, :], func=AF.Silu), waits=[tA, fs])
                        banks.release(kA, t_s)
                        t_m = P.op("dve", lambda e, bB=bB, sbuf_=sbuf_, j=j, th=th: e.tensor_tensor(
                            out=aT[:, j, th * 512:(th + 1) * 512], in0=sbuf_[:, :], in1=bB[:, :], op=ALU.mult),
                            waits=[t_s, tB, aT_free_holder[0]])
                        banks.release(kB, t_m)
                        silu.release(ks, t_m)
                        aT_ready.append(t_m)
                wa.release(ka, last_pe)
                wb.release(kb, last_pe)
            last_mm = None
            for n in range(D // 512):
                acc = [banks.next() for _ in range(NTT)]
                for jb in range(NFF // 4):
                    kw, wbuf, fw = w2r.next()
                    tw = P.dma("pool", wbuf[:, :, :],
                               w2_d[jb * 512:(jb + 1) * 512, n * 512:(n + 1) * 512].rearrange("(q p) c -> p q c", p=128),
                               s_w2[kw], waits=[fw])
                    for q in range(4):
                        j = jb * 4 + q
                        for tt in range(NTT):
                            kb_, bk, fb_ = acc[tt]
                            w_ = []
                            if j == 0:
                                w_ = [fb_, aT_ready]
                            if q == 0 and tt == 0:
                                w_ = w_ + [tw]
                            last_mm = P.op("pe", lambda e, bk=bk, j=j, tt=tt, wbuf=wbuf, q=q: e.matmul(
                                bk[:, :], lhsT=aT[:, j, tt * 128:(tt + 1) * 128], rhs=wbuf[:, q, :],
                                start=(j == 0), stop=(j == NFF - 1)),
                                waits=w_, sig=(j == NFF - 1) or (q == 3 and tt == NTT - 1))
                            if j == NFF - 1:
                                acc[tt] = (kb_, bk, last_mm)
                    w2r.release(kw, last_mm)
                for tt in range(NTT):
                    kb_, bk, t_acc = acc[tt]
                    kr, xr, fr = xres.next()
                    rows = slice(tok0 + tt * 128, tok0 + (tt + 1) * 128)
                    t_r = P.dma("sp", xr[:, :], x_src[rows, n * 512:(n + 1) * 512], s_xr[kr], waits=[fr])
                    ko, o, fo = xo.next()
                    t_e = P.op("dve", lambda e, o=o, bk=bk, xr=xr: e.scalar_tensor_tensor(
                        out=o[:, :], in0=bk[:, :], scalar=0.5, in1=xr[:, :], op0=ALU.mult, op1=ALU.add),
                        waits=[t_acc, t_r, fo])
                    banks.release(kb_, t_e)
                    xres.release(kr, t_e)
                    t_st = P.dma("sp", x_dst[rows, n * 512:(n + 1) * 512], o[:, :], s_xo[ko], waits=[t_e])
                    xo.release(ko, t_st)
            aT_free_holder[0] = last_mm
        P.raw("sp", lambda e: None, waits=[xo.free[i] for i in range(2)])
        P.emit()


def coll_allgather(P, src_t, dst_t, waits):
    P._waits("pool", waits)
    toks = []
    for k in range(4):
        nag = P.reg.setdefault("nag", 0)
        P.reg["nag"] = nag + 1
        sem = P.newsem("ag%d" % nag)
        P.cnt[sem] += 1

        def fn(e, k=k):
            return e.collective_compute("AllGather", ALU.bypass,
                                        replica_groups=[[0, 1], [2, 3], [4, 5], [6, 7]],
                                        ins=[src_t.ap()[k * 512:(k + 1) * 512, :]],
                                        outs=[dst_t.ap()[k * 1024:(k + 1) * 1024, :]])
        P.q["pool"].append(("c", fn, sem))
        toks.append((sem, 1))
    return toks


def normT_phase(nc, tag, C, x_src, gain_t, hT_in_t, hT_all_t):
    with contextlib.ExitStack() as es:
        P = Prog(nc, es, tag)
        stage = sb(nc, es, tag + "stage", [128, NCH, NTOK], BF16)
        xin = Ring([sb(nc, es, tag + "xin%d" % i, [128, D], F32) for i in range(2)])
        s_xin = [P.slot_sem() for _ in range(2)]
        banks = Ring([ps(nc, es, tag + "ps%d" % i, [128, 512]) for i in range(4)])
        NC_ = NormCtx(nc, es, P, tag, bcast_rows(gain_t, 0, D), C["ident"], banks)
        NT = NTOK // 128
        cp = []

        def load_x(tt):
            k, buf, fr = xin.next()
            t = P.dma("sp", buf[:, :], x_src[tt * 128:(tt + 1) * 128, :], s_xin[k], waits=[fr])
            return k, buf, t
        nxt = load_x(0)
        for tt in range(NT):
            k, buf, t = nxt
            if tt + 1 < NT:
                nxt = load_x(tt + 1)

            def dst(half, pTb, tk, tt=tt):
                return P.op("act", lambda e: e.copy(
                    out=stage[:, half * 8:(half + 1) * 8, tt * 128:(tt + 1) * 128],
                    in_=pTb.rearrange("p (c t) -> p c t", c=8)), waits=[tk])
            toks, t_free = NC_.tile(buf, t, dst)
            xin.release(k, t_free)
            cp += toks
        s_out = P.slot_sem()
        hv = hT_in_t.ap()
        t_o = None
        for c in range(NCH):
            t_o = P.dma("sp", hv[c * 128:(c + 1) * 128, :], stage[:, c, :], s_out, waits=[cp] if c == 0 else ())
        t_ag = coll_allgather(P, hT_in_t, hT_all_t, [t_o])
        P.raw("pool", lambda e: None, waits=[t_ag])
        P.emit()


def wout_phase(nc, tag, C, yT_all_t, wo_d, x1, x2):
    with contextlib.ExitStack() as es:
        P = Prog(nc, es, tag)
        wo = sb(nc, es, tag + "wo", [128, NCH, D], BF16)
        yT = sb(nc, es, tag + "yT", [128, NCH, NTOK], BF16)
        xres = Ring([sb(nc, es, tag + "xres%d" % i, [128, 512], F32) for i in range(3)])
        s_xr = [P.slot_sem() for _ in range(3)]
        xo = Ring([sb(nc, es, tag + "xo%d" % i, [128, 512], F32) for i in range(3)])
        s_xo = [P.slot_sem() for _ in range(3)]
        banks = Ring([ps(nc, es, tag + "ps%d" % i, [128, 512]) for i in range(8)])
        s_w = P.slot_sem()
        s_y = P.slot_sem()
        wov = wo_d.rearrange("(c p) n -> p c n", p=128)
        t_w = None
        for c4 in range(4):
            t_w = P.dma("pool", wo[:, c4 * 4:(c4 + 1) * 4, :], wov[:, c4 * 4:(c4 + 1) * 4, :], s_w)
        yv = yT_all_t.ap()

        rk = {}

        def ld(e, i):
            if "r" not in rk:
                rk["r"] = e.partition_id() % 2
            rank = rk["r"]
            ii, r = i // 2, i % 2
            off = ii * 1024 + r * 512
            start = rank * 2048 + off if off else rank * 2048
            return e.dma_start(out=yT[:, i * 4:(i + 1) * 4, :],
                               in_=yv[bass.ds(start, 512), :].rearrange("(cc p) t -> p cc t", p=128))
        t_y = None
        for i in range(4):
            P.cnt[s_y] += 16
            P.q["pool"].append(("i", lambda e, i=i: ld(e, i), s_y, 16))
            t_y = (s_y, P.cnt[s_y])
        for tt in range(NTOK // 128):
            for n in range(D // 512):
                kb, bk, fb = banks.next()
                for c in range(NCH):
                    t_mm = P.op("pe", lambda e, c=c, bk=bk, tt=tt, n=n: e.matmul(
                        bk[:, :], lhsT=yT[:, c, tt * 128:(tt + 1) * 128], rhs=wo[:, c, n * 512:(n + 1) * 512],
                        start=(c == 0), stop=(c == NCH - 1)),
                        waits=[fb, t_w, t_y] if c == 0 else (), sig=(c == NCH - 1))
                kr, xr, fr = xres.next()
                rows = slice(tt * 128, (tt + 1) * 128)
                t_r = P.dma("sp", xr[:, :], x1[rows, n * 512:(n + 1) * 512], s_xr[kr], waits=[fr])
                ko, o, fo = xo.next()
                t_e = P.op("dve", lambda e, o=o, bk=bk, xr=xr: e.tensor_tensor(
                    out=o[:, :], in0=bk[:, :], in1=xr[:, :], op=ALU.add), waits=[t_mm, t_r, fo])
                banks.release(kb, t_e)
                xres.release(kr, t_e)
                t_st = P.dma("sp", x2[rows, n * 512:(n + 1) * 512], o[:, :], s_xo[ko], waits=[t_e])
                xo.release(ko, t_st)
        P.raw("sp", lambda e: None, waits=[xo.free[i] for i in range(3)])
        P.emit()


def final_phase(nc, tag, x_src, gain_t, out):
    with contextlib.ExitStack() as es:
        P = Prog(nc, es, tag)
        gain_bc = sb(nc, es, tag + "gain", [128, D], F32)
        ss = sb(nc, es, tag + "ss", [128, 64], F32)
        rstd = sb(nc, es, tag + "rstd", [128, 64], F32)
        junk = sb(nc, es, tag + "junk", [128, D], BF16)
        xin = Ring([sb(nc, es, tag + "xin%d" % i, [128, D], F32) for i in range(3)])
        s_xin = [P.slot_sem() for _ in range(3)]
        xo = Ring([sb(nc, es, tag + "xo%d" % i, [128, D], F32) for i in range(3)])
        s_xo = [P.slot_sem() for _ in range(3)]
        s_c = P.slot_sem()
        t_g = P.dma("sp", gain_bc[:, :], bcast_rows(gain_t, 0, D), s_c)
        prev_sq = None
        for tt in range(NTOK // 128):
            k, xt, fr = xin.next()
            t = P.dma("sp", xt[:, :], x_src[tt * 128:(tt + 1) * 128, :], s_xin[k], waits=[fr])
            col = tt
            t_sq = P.op("act", lambda e, xt=xt, col=col: e.activation(out=junk[:, :], in_=xt[:, :], func=AF.Square,
                                                                      accum_out=ss[:, col:col + 1]), waits=[t])
            t_a = P.op("dve", lambda e, col=col: e.tensor_scalar(out=rstd[:, col:col + 1], in0=ss[:, col:col + 1],
                                                                 scalar1=1.0 / D, scalar2=EPS, op0=ALU.mult, op1=ALU.add),
                       waits=[t_sq])
            t_a2 = P.op("act", lambda e, col=col: e.sqrt(out=ss[:, col:col + 1], in_=rstd[:, col:col + 1]), waits=[t_a])
            t_b = P.op("dve", lambda e, col=col: e.reciprocal(out=rstd[:, col:col + 1], in_=ss[:, col:col + 1]), waits=[t_a2])
            ko, o, fo = xo.next()
            t_n = P.op("dve", lambda e, o=o, xt=xt, col=col: e.scalar_tensor_tensor(
                out=o[:, :], in0=xt[:, :], scalar=rstd[:, col:col + 1], in1=gain_bc[:, :], op0=ALU.mult, op1=ALU.mult),
                waits=[t_b, fo, t_g])
            xin.release(k, t_n)
            t_st = P.dma("sp", out[tt * 128:(tt + 1) * 128, :], o[:, :], s_xo[ko], waits=[t_n])
            xo.release(ko, t_st)
        P.raw("sp", lambda e: None, waits=[xo.free[i] for i in range(3)])
        P.emit()


def scrub_phase(nc, tag):
    with contextlib.ExitStack() as es:
        P = Prog(nc, es, tag)
        big = [sb(nc, es, tag + "big%d" % i, [128, 12800], F32) for i in range(4)]
        banks = [ps(nc, es, tag + "ps%d" % i, [128, 512]) for i in range(8)]
        for i, b_ in enumerate(big):
            P.op("pool" if i % 2 else "dve", lambda e, b_=b_: e.memset(b_[:, :], 0.0))
        for b_ in banks:
            P.op("dve", lambda e, b_=b_: e.memset(b_[:, :], 0.0))
        P.emit()


def stub_mixer_phase(nc, tag, yT_in_t, yT_all_t, zero_lo=0, zero_hi=1024, dbg=None):
    with contextlib.ExitStack() as es:
        P = Prog(nc, es, tag)
        z = sb(nc, es, tag + "z", [128, NTOK], BF16)
        t_z = P.op("pool", lambda e: e.memset(z[:, :], 0.0))
        s = P.slot_sem()
        t = None
        yv = yT_in_t.ap()
        for hf in range(2):
            for c in range(zero_lo // 128, zero_hi // 128):
                t = P.dma("sp", yv[hf * 1024 + c * 128: hf * 1024 + (c + 1) * 128, :], z[:, :], s, waits=[t_z])
        t_ag = coll_allgather(P, yT_in_t, yT_all_t, [t, t_z])
        P.raw("pool", lambda e: None, waits=[t_ag])
        if dbg is not None:
            sd_ = P.slot_sem()
            td_ = None
            for (dst_, src_) in dbg:
                td_ = P.dma("sp", dst_, src_, sd_, waits=[t_ag])
            P.raw("sp", lambda e: None, waits=[td_])
        P.emit()


STAGE = 3


def build_program(stage=STAGE):
    nc = bass.Bass("TRN2", target_bir_lowering=False)

    def inp(name, shape, dt=F32):
        return nc.dram_tensor(name, shape, dt, kind="ExternalInput")
    x = inp("x", [NTOK, D]).ap()
    n1 = inp("ffn1_norm", [1, D])
    a_w1 = inp("ffn1_w1", [D, DFF]).ap()
    a_w3 = inp("ffn1_w3", [D, DFF]).ap()
    a_w2 = inp("ffn1_w2", [DFF, D]).ap()
    nm = inp("mix_norm", [1, D])
    n2 = inp("ffn2_norm", [1, D])
    c_w1 = inp("ffn2_w1", [D, DFF]).ap()
    c_w3 = inp("ffn2_w3", [D, DFF]).ap()
    c_w2 = inp("ffn2_w2", [DFF, D]).ap()
    nf = inp("final_norm", [1, D])
    wo = inp("w_out_p", [D, D]).ap()
    C = {"ident": inp("ident", [128, 128], BF16).ap()}
    mix_in = declare_mixer_inputs(nc, inp) if stage >= 2 else None
    if stage == 4:
        stage = 3
        s4 = True
    else:
        s4 = False
    out = nc.dram_tensor("out", [NTOK, D], F32, kind="ExternalOutput").ap()

    x1 = nc.dram_tensor("x1", [NTOK, D], F32).ap()
    x2 = nc.dram_tensor("x2", [NTOK, D], F32).ap()
    x3 = nc.dram_tensor("x3", [NTOK, D], F32).ap()
    hT_in = nc.dram_tensor("hT_in", [D, NTOK], BF16)
    hT_all = nc.dram_tensor("hT_all", [2 * D, NTOK], BF16)
    yT_in = nc.dram_tensor("yT_in", [2048, NTOK], BF16)
    yT_all = nc.dram_tensor("yT_all", [4096, NTOK], BF16)

    ffn_phase(nc, "A", C, x, n1, a_w1, a_w3, a_w2, x1)
    normT_phase(nc, "B", C, x1, nm, hT_in, hT_all)
    if stage == 1:
        stub_mixer_phase(nc, "S", yT_in, yT_all)
    else:
        rdbg = None
        if stage == 2:
            rdbg = {}
            for b_ in (0, 1):
                rdbg["qk%d" % b_] = nc.dram_tensor("dbg_qk%d" % b_, [128, 2, 2, 2, 512], BF16, kind="ExternalOutput").ap()
                rdbg["qx%d" % b_] = nc.dram_tensor("dbg_qx%d" % b_, [128, 2, 2, 512], BF16, kind="ExternalOutput").ap()
                rdbg["R%d" % b_] = nc.dram_tensor("dbg_R%d" % b_, [128, 2, 512], F32, kind="ExternalOutput").ap()
                rdbg["Rb%d" % b_] = nc.dram_tensor("dbg_Rb%d" % b_, [128, 2, 512], BF16, kind="ExternalOutput").ap()
                rdbg["v%d" % b_] = nc.dram_tensor("dbg_v%d" % b_, [128, 4, 512], BF16, kind="ExternalOutput").ap()
        ret_phase(nc, "R", C, mix_in, hT_all, yT_in, dbg=rdbg)
        if stage == 2:
            d1 = nc.dram_tensor("dbg_yT", [2048, NTOK], BF16, kind="ExternalOutput").ap()
            d2 = nc.dram_tensor("dbg_hT", [4096, NTOK], BF16, kind="ExternalOutput").ap()
            stub_mixer_phase(nc, "S", yT_in, yT_all, 512, 1024, dbg=[(d1, yT_in.ap()), (d2, hT_all.ap())])
        else:
            scrub_phase(nc, "Z")
            nsa_phase(nc, "N", C, mix_in, hT_all, yT_in, yT_all, do_ag=False)
            stub_mixer_phase(nc, "S", yT_in, yT_all, 0, 0)
    wout_phase(nc, "W", C, yT_all, wo, x1, x2)
    if s4:
        final_phase(nc, "F", x2, nf, out)
        return nc
    ffn_phase(nc, "C", C, x2, n2, c_w1, c_w3, c_w2, x3)
    final_phase(nc, "F", x3, nf, out)
    return nc


_NC_CACHE = {}


def host_consts():
    import ml_dtypes
    c = {"ident": np.eye(128, dtype=np.float32).astype(ml_dtypes.bfloat16)}
    return c


def build_launch(which):
    nc = bass.Bass("TRN2", target_bir_lowering=False)

    def inp(name, shape, dt=F32):
        return nc.dram_tensor(name, shape, dt, kind="ExternalInput")
    C = {"ident": inp("ident", [128, 128], BF16).ap()}
    if which == "L1":
        x = inp("x", [NTOK, D]).ap()
        n1 = inp("ffn1_norm", [1, D])
        w1 = inp("ffn1_w1", [D, DFF]).ap()
        w3 = inp("ffn1_w3", [D, DFF]).ap()
        w2 = inp("ffn1_w2", [DFF, D]).ap()
        out = nc.dram_tensor("out", [NTOK, D], F32, kind="ExternalOutput").ap()
        ffn_phase(nc, "A", C, x, n1, w1, w3, w2, out)
    elif which == "L23":
        x1 = inp("x", [NTOK, D]).ap()
        nm = inp("mix_norm", [1, D])
        wo = inp("w_out_p", [D, D]).ap()
        mix_in = declare_mixer_inputs(nc, inp)
        n2 = inp("ffn2_norm", [1, D])
        w1 = inp("ffn2_w1", [D, DFF]).ap()
        w3 = inp("ffn2_w3", [D, DFF]).ap()
        w2 = inp("ffn2_w2", [DFF, D]).ap()
        nf = inp("final_norm", [1, D])
        out = nc.dram_tensor("out", [NTOK, D], F32, kind="ExternalOutput").ap()
        hT_in = nc.dram_tensor("hT_in", [D, NTOK], BF16)
        hT_all = nc.dram_tensor("hT_all", [2 * D, NTOK], BF16)
        yT_in = nc.dram_tensor("yT_in", [2048, NTOK], BF16)
        yT_all = nc.dram_tensor("yT_all", [4096, NTOK], BF16)
        x2 = nc.dram_tensor("x2", [NTOK, D], F32).ap()
        x3 = nc.dram_tensor("x3", [NTOK, D], F32).ap()
        normT_phase(nc, "B", C, x1, nm, hT_in, hT_all)
        ret_phase(nc, "R", C, mix_in, hT_all, yT_in)
        nsa_phase(nc, "N", C, mix_in, hT_all, yT_in, yT_all)
        wout_phase(nc, "W", C, yT_all, wo, x1, x2)
        ffn_phase(nc, "C", C, x2, n2, w1, w3, w2, x3)
        final_phase(nc, "F", x3, nf, out)
    elif which == "L2":
        x1 = inp("x", [NTOK, D]).ap()
        nm = inp("mix_norm", [1, D])
        wo = inp("w_out_p", [D, D]).ap()
        mix_in = declare_mixer_inputs(nc, inp)
        out = nc.dram_tensor("out", [NTOK, D], F32, kind="ExternalOutput").ap()
        hT_in = nc.dram_tensor("hT_in", [D, NTOK], BF16)
        hT_all = nc.dram_tensor("hT_all", [2 * D, NTOK], BF16)
        yT_in = nc.dram_tensor("yT_in", [2048, NTOK], BF16)
        yT_all = nc.dram_tensor("yT_all", [4096, NTOK], BF16)
        normT_phase(nc, "B", C, x1, nm, hT_in, hT_all)
        ret_phase(nc, "R", C, mix_in, hT_all, yT_in)
        nsa_phase(nc, "N", C, mix_in, hT_all, yT_in, yT_all)
        wout_phase(nc, "W", C, yT_all, wo, x1, out)
    else:
        x2 = inp("x", [NTOK, D]).ap()
        n2 = inp("ffn2_norm", [1, D])
        w1 = inp("ffn2_w1", [D, DFF]).ap()
        w3 = inp("ffn2_w3", [D, DFF]).ap()
        w2 = inp("ffn2_w2", [DFF, D]).ap()
        nf = inp("final_norm", [1, D])
        out = nc.dram_tensor("out", [NTOK, D], F32, kind="ExternalOutput").ap()
        x3 = nc.dram_tensor("x3", [NTOK, D], F32).ap()
        ffn_phase(nc, "C", C, x2, n2, w1, w3, w2, x3)
        final_phase(nc, "F", x3, nf, out)
    return nc


SPLIT = False
LAUNCHES = ("L1", "L23")


def kernel(x, ffn1_norm, ffn1_w1, ffn1_w3, ffn1_w2, mix_norm, w_in, ret_gn_gain,
           cmp_pe_k, cmp_w1_k, cmp_w2_k, cmp_pe_v, cmp_w1_v, cmp_w2_v, w_out,
           ffn2_norm, ffn2_w1, ffn2_w3, ffn2_w2, rel_bias, final_norm, _stage=None):
    stage = STAGE if _stage is None else _stage
    f = lambda a: np.ascontiguousarray(np.asarray(a, dtype=np.float32))
    x = f(x)
    w_in0 = f(w_in)[0]
    w_out0 = f(w_out)[0]
    consts = host_consts()
    if SPLIT and _stage is None:
        xs = [np.ascontiguousarray(x[c // 2, (c % 2) * NTOK:(c % 2 + 1) * NTOK]) for c in range(8)]
        for which in LAUNCHES:
            if which not in _NC_CACHE:
                _NC_CACHE[which] = build_launch(which)
            nc = _NC_CACHE[which]
            if which == "L1":
                common = {"ffn1_norm": f(ffn1_norm).reshape(1, D), "ffn1_w1": f(ffn1_w1)[0], "ffn1_w3": f(ffn1_w3)[0],
                          "ffn1_w2": f(ffn1_w2)[0]}
            elif which == "L2":
                common = {"mix_norm": f(mix_norm).reshape(1, D), "w_out_p": w_out0}
            elif which == "L23":
                common = {"mix_norm": f(mix_norm).reshape(1, D), "w_out_p": w_out0,
                          "ffn2_norm": f(ffn2_norm).reshape(1, D), "ffn2_w1": f(ffn2_w1)[0], "ffn2_w3": f(ffn2_w3)[0],
                          "ffn2_w2": f(ffn2_w2)[0], "final_norm": f(final_norm).reshape(1, D)}
            else:
                common = {"ffn2_norm": f(ffn2_norm).reshape(1, D), "ffn2_w1": f(ffn2_w1)[0], "ffn2_w3": f(ffn2_w3)[0],
                          "ffn2_w2": f(ffn2_w2)[0], "final_norm": f(final_norm).reshape(1, D)}
            common.update(consts)
            in_maps = []
            for c in range(8):
                m = dict(common)
                m["x"] = xs[c]
                if which in ("L2", "L23"):
                    m.update(mixer_host_inputs(c % 2, w_in0, f(ret_gn_gain)[0], f(cmp_pe_k)[0], f(cmp_w1_k)[0], f(cmp_w2_k)[0],
                                               f(cmp_pe_v)[0], f(cmp_w1_v)[0], f(cmp_w2_v)[0], f(rel_bias)))
                in_maps.append(m)
            res = run_bass_kernel_spmd(nc, in_maps, core_ids=list(range(8)))
            xs = [np.ascontiguousarray(np.asarray(res.results[c]["out"], dtype=np.float32)) for c in range(8)]
        outp = np.empty((4, SEQ, D), np.float32)
        for c in range(8):
            outp[c // 2, (c % 2) * NTOK:(c % 2 + 1) * NTOK] = xs[c]
        return outp
    if stage not in _NC_CACHE:
        _NC_CACHE[stage] = build_program(stage)
    nc = _NC_CACHE[stage]
    perm = np.arange(2048)
    common = {
        "ffn1_norm": f(ffn1_norm).reshape(1, D), "ffn1_w1": f(ffn1_w1)[0], "ffn1_w3": f(ffn1_w3)[0], "ffn1_w2": f(ffn1_w2)[0],
        "mix_norm": f(mix_norm).reshape(1, D),
        "ffn2_norm": f(ffn2_norm).reshape(1, D), "ffn2_w1": f(ffn2_w1)[0], "ffn2_w3": f(ffn2_w3)[0], "ffn2_w2": f(ffn2_w2)[0],
        "final_norm": f(final_norm).reshape(1, D),
        "w_out_p": np.ascontiguousarray(w_out0[perm]),
    }
    common.update(consts)
    in_maps = []
    for c in range(8):
        b, j = c // 2, c % 2
        m = dict(common)
        m["x"] = np.ascontiguousarray(x[b, j * NTOK:(j + 1) * NTOK])
        if stage >= 2:
            m.update(mixer_host_inputs(j, w_in0, f(ret_gn_gain)[0], f(cmp_pe_k)[0], f(cmp_w1_k)[0], f(cmp_w2_k)[0],
                                       f(cmp_pe_v)[0], f(cmp_w1_v)[0], f(cmp_w2_v)[0], f(rel_bias)))
        in_maps.append(m)
    res = run_bass_kernel_spmd(nc, in_maps, core_ids=list(range(8)))
    global _LAST_RES
    _LAST_RES = res
    outp = np.empty((4, SEQ, D), np.float32)
    for c in range(8):
        b, j = c // 2, c % 2
        outp[b, j * NTOK:(j + 1) * NTOK] = res.results[c]["out"]
    return outp


def declare_mixer_inputs(nc, inp):
    m = {}
    m["w_ret"] = inp("w_ret", [D, 2048]).ap()
    m["gn_gain"] = inp("gn_gain", [1, 512])
    m["cosT"] = inp("cosT", [128, SEQ]).ap()
    m["sinT"] = inp("sinT", [128, SEQ]).ap()
    m["decT"] = inp("decT", [128, 2, 128]).ap()
    m["xi4"] = inp("xi4", [128, 2, 512]).ap()
    m["zeta"] = inp("zeta", [128, 2]).ap()
    m["cdec"] = inp("cdec", [128, 2]).ap()
    nsa_declare(nc, inp, m)
    return m


def ret_host_inputs(g, w_in0, gn_gain):
    hs = [2 * g, 2 * g + 1]
    cols = []
    for base in (0, 1024, 2048, 3072):
        for h in hs:
            cols.append(np.arange(base + h * 256, base + (h + 1) * 256))
    cols = np.concatenate(cols)
    m = {"w_ret": np.ascontiguousarray(w_in0[:, cols]),
         "gn_gain": np.ascontiguousarray(gn_gain[hs[0] * 256:(hs[1] + 1) * 256].reshape(1, 512))}
    half = 128
    inv = (10000.0 ** (-np.arange(half, dtype=np.float32) / np.float32(half))).astype(np.float32)
    ang = np.arange(SEQ, dtype=np.float32)[:, None] * inv[None, :]
    m["cosT"] = np.ascontiguousarray(np.cos(ang).T.astype(np.float32))
    m["sinT"] = np.ascontiguousarray(np.sin(ang).T.astype(np.float32))
    ks = 256.0 ** -0.5
    decT = np.zeros((128, 2, 128), np.float32)
    xi4 = np.zeros((128, 2, 512), np.float32)
    zeta = np.zeros((128, 2), np.float32)
    cdec = np.zeros((128, 2), np.float32)
    idx = np.arange(128, dtype=np.float64)
    for i, h in enumerate(hs):
        lg = np.log(1.0 - 2.0 ** (-5.0 - h))
        diff = idx[None, :] - idx[:, None]
        decT[:, i, :] = np.where(diff >= 0, np.exp(np.maximum(diff, 0) * lg), 0.0) * ks
        xi4[:, i, :] = np.tile(np.exp((idx + 1.0) * lg), 4)[None, :]
        zeta[:, i] = np.exp((127.0 - idx) * lg) * ks
        cdec[:, i] = np.exp(128.0 * lg)
    m.update({"decT": decT, "xi4": xi4, "zeta": zeta, "cdec": cdec})
    return m


def load_hT_block(P, eng, buf, hv, r, col0, bt, sem, waits):
    t = None
    for k in range(4):
        t = P.dma(eng, buf[:, k * 4:(k + 1) * 4, :],
                  hv[k * 1024 + r * 512: k * 1024 + (r + 1) * 512, col0:col0 + bt].rearrange("(c p) t -> p c t", p=128),
                  sem, waits=waits if k == 0 else ())
    return t


def ret_phase(nc, tag, C, M, hT_all_t, yT_in_t, dbg=None):
    BT = 512
    with contextlib.ExitStack() as es:
        P = Prog(nc, es, tag)
        w = sb(nc, es, tag + "w", [128, NCH, 2048], BF16)
        hc = Ring([sb(nc, es, tag + "hc%d" % i, [128, NCH, BT], BF16) for i in range(2)])
        s_hc = [P.slot_sem() for _ in range(2)]
        cs = Ring([sb(nc, es, tag + "cs%d" % i, [128, 2, BT], F32) for i in range(2)])
        s_cs = [P.slot_sem() for _ in range(2)]
        ident = sb(nc, es, tag + "ident", [128, 128], BF16)
        gain_bc = sb(nc, es, tag + "gain", [128, 512], F32)
        decT = sb(nc, es, tag + "decT", [128, 2, 128], F32)
        xi4 = sb(nc, es, tag + "xi4", [128, 2, 512], F32)
        zeta = sb(nc, es, tag + "zeta", [128, 2], F32)
        cdec = sb(nc, es, tag + "cdec", [128, 2], F32)
        qk = sb(nc, es, tag + "qk", [128, 2, 2, 2, BT], BF16)
        qx = sb(nc, es, tag + "qx", [128, 2, 2, BT], BF16)
        vsb = sb(nc, es, tag + "v", [128, 4, 512], BF16)
        gs = sb(nc, es, tag + "gs", [128, 512], F32)
        gsg = sb(nc, es, tag + "gsg", [128, 4, 512], F32)
        tmp = [sb(nc, es, tag + "tmp%d" % i, [128, BT], F32) for i in range(4)]
        R32 = sb(nc, es, tag + "R32", [128, 2, 512], F32)
        Rb = sb(nc, es, tag + "Rb", [128, 2, 512], BF16)
        sd = Ring([sb(nc, es, tag + "sd%d" % i, [128, 128], BF16) for i in range(2)])
        kz = Ring([sb(nc, es, tag + "kz%d" % i, [128, 256], BF16) for i in range(2)])
        st6 = sb(nc, es, tag + "st6", [128, 8, 6], F32)
        mv = sb(nc, es, tag + "mv", [128, 8, 4], F32)
        yn = Ring([sb(nc, es, tag + "yn%d" % i, [128, 256], F32) for i in range(2)])
        yo = Ring([sb(nc, es, tag + "yo%d" % i, [128, 256], BF16) for i in range(2)])
        ystage = Ring([sb(nc, es, tag + "ys%d" % i, [128, 4, BT], BF16) for i in range(2)])
        s_ys = [P.slot_sem() for _ in range(2)]
        banks = Ring([ps(nc, es, tag + "ps%d" % i, [128, 512]) for i in range(8)])

        s_c = P.slot_sem()
        wv = M["w_ret"].rearrange("(c p) n -> p c n", p=128)
        s_w = P.slot_sem()
        t_w = None
        for c4 in range(4):
            t_w = P.dma("pool", w[:, c4 * 4:(c4 + 1) * 4, :], wv[:, c4 * 4:(c4 + 1) * 4, :], s_w)
        P.dma("sp", ident[:, :], C["ident"], s_c)
        P.dma("sp", gain_bc[:, :], bcast_rows(M["gn_gain"], 0, 512), s_c)
        P.dma("sp", decT[:, :, :], M["decT"], s_c)
        P.dma("sp", xi4[:, :, :], M["xi4"], s_c)
        P.dma("sp", zeta[:, :], M["zeta"], s_c)
        t_c = P.dma("sp", cdec[:, :], M["cdec"], s_c)
        t_r0 = P.op("pool", lambda e: e.memset(R32[:, :, :], 0.0))
        t_rb0 = P.op("pool", lambda e: e.memset(Rb[:, :, :], 0.0))
        hv = hT_all_t.ap()
        yv = yT_in_t.ap()
        NBLK = SEQ // BT

        def load_blk(b):
            T0 = b * BT
            r, col0 = T0 // NTOK, T0 % NTOK
            k, buf, fr = hc.next()
            t1 = load_hT_block(P, "sp", buf, hv, r, col0, BT, s_hc[k], [fr])
            k2, cb, fr2 = cs.next()
            P.dma("sp", cb[:, 0, :], M["cosT"][:, T0:T0 + BT], s_cs[k2], waits=[fr2])
            t2 = P.dma("sp", cb[:, 1, :], M["sinT"][:, T0:T0 + BT], s_cs[k2])
            return k, buf, t1, k2, cb, t2

        Rb_ready = [t_rb0, t_rb0]
        R32_ready = [t_r0, t_r0]
        qk_free = [None]
        tmp_free = [None] * 4
        nxt = load_blk(0)
        for b in range(NBLK):
            k, hcb, t_h, k2, cb, t_cs = nxt
            if b + 1 < NBLK:
                nxt = load_blk(b + 1)
            T0 = b * BT
            last_proj = None
            rot_done = []
            for which in range(2):
                for h in range(2):
                    bk = []
                    for dc in range(2):
                        col0 = which * 512 + h * 256 + dc * 128
                        kb, bank, fb = banks.next()
                        for c in range(NCH):
                            tm = P.op("pe", lambda e, c=c, bank=bank, col0=col0, hcb=hcb: e.matmul(
                                bank[:, :], lhsT=w[:, c, col0:col0 + 128], rhs=hcb[:, c, :],
                                start=(c == 0), stop=(c == NCH - 1)),
                                waits=[t_w, t_h, fb] if c == 0 else (), sig=(c == NCH - 1))
                        bk.append((kb, bank, tm))
                    last_proj = bk[1][2]
                    (k1_, x1, tx1), (k2_, x2, tx2) = bk
                    o1 = qk[:, which, h, 0, :]
                    o2 = qk[:, which, h, 1, :]
                    ta = P.op("dve", lambda e, x1=x1: e.tensor_tensor(out=tmp[0][:, :], in0=x1[:, :], in1=cb[:, 0, :], op=ALU.mult),
                              waits=[tx1, t_cs, tmp_free[0]])
                    tb = P.op("dve", lambda e, x2=x2: e.tensor_tensor(out=tmp[1][:, :], in0=x2[:, :], in1=cb[:, 1, :], op=ALU.mult),
                              waits=[tx2, tmp_free[1]])
                    tc_ = P.op("dve", lambda e, x1=x1: e.tensor_tensor(out=tmp[2][:, :], in0=x1[:, :], in1=cb[:, 1, :], op=ALU.mult),
                               waits=[tmp_free[2]])
                    td = P.op("dve", lambda e, x2=x2: e.tensor_tensor(out=tmp[3][:, :], in0=x2[:, :], in1=cb[:, 0, :], op=ALU.mult),
                              waits=[tmp_free[3]])
                    banks.release(k1_, td)
                    banks.release(k2_, td)
                    te = P.op("pool", lambda e, o1=o1: e.tensor_tensor(out=o1, in0=tmp[0][:, :], in1=tmp[1][:, :], op=ALU.subtract),
                              waits=[ta, tb, qk_free[0]])
                    tf = P.op("pool", lambda e, o2=o2: e.tensor_tensor(out=o2, in0=tmp[2][:, :], in1=tmp[3][:, :], op=ALU.add),
                              waits=[tc_, td])
                    tmp_free[0] = tmp_free[1] = te
                    tmp_free[2] = tmp_free[3] = tf
                    rot_done += [te, tf]
                    if which == 0:
                        for dc, tsrc in ((0, te), (1, tf)):
                            tq = P.op("pool", lambda e, dc=dc, h=h: e.tensor_tensor(
                                out=qx[:, h, dc, :], in0=qk[:, 0, h, dc, :], in1=xi4[:, h, :], op=ALU.mult),
                                waits=[tsrc, t_c])
                            rot_done.append(tq)
            vg_done = []
            for t in range(4):
                kb, bank, fb = banks.next()
                for c in range(NCH):
                    tm = P.op("pe", lambda e, c=c, bank=bank, t=t, hcb=hcb: e.matmul(
                        bank[:, :], lhsT=hcb[:, c, t * 128:(t + 1) * 128], rhs=w[:, c, 1024:1536],
                        start=(c == 0), stop=(c == NCH - 1)), waits=[fb] if c == 0 else (), sig=(c == NCH - 1))
                tv = P.op("act", lambda e, bank=bank, t=t: e.copy(out=vsb[:, t, :], in_=bank[:, :]), waits=[tm, qk_free[0]])
                banks.release(kb, tv)
                kb, bank, fb = banks.next()
                for c in range(NCH):
                    tm = P.op("pe", lambda e, c=c, bank=bank, t=t, hcb=hcb: e.matmul(
                        bank[:, :], lhsT=hcb[:, c, t * 128:(t + 1) * 128], rhs=w[:, c, 1536:2048],
                        start=(c == 0), stop=(c == NCH - 1)), waits=[fb] if c == 0 else (), sig=(c == NCH - 1))
                last_proj = tm
                tg = P.op("act", lambda e, bank=bank: e.activation(out=gs[:, :], in_=bank[:, :], func=AF.Silu),
                          waits=[tm, vg_done[-1] if vg_done else None])
                banks.release(kb, tg)
                tg2 = P.op("dve", lambda e, t=t: e.tensor_tensor(out=gsg[:, t, :], in0=gs[:, :], in1=gain_bc[:, :], op=ALU.mult),
                           waits=[tg, t_c, qk_free[0]])
                vg_done += [tv, tg2]
            hc.release(k, last_proj)
            cs.release(k2, rot_done[-1])
            if dbg is not None and b in (0, 1):
                sdb_ = P.slot_sem()
                P.dma("sp", dbg["qk%d" % b], qk[:, :, :, :, :], sdb_, waits=[rot_done, vg_done])
                P.dma("sp", dbg["qx%d" % b], qx[:, :, :, :], sdb_)
                P.dma("sp", dbg["R%d" % b], R32[:, :, :], sdb_, waits=[R32_ready])
                P.dma("sp", dbg["Rb%d" % b], Rb[:, :, :], sdb_, waits=[Rb_ready])
                tdb_ = P.dma("sp", dbg["v%d" % b], vsb[:, :, :], sdb_)
                P.raw("sp", lambda e: None, waits=[tdb_])
                P.raw("pe", lambda e: None, waits=[tdb_])
                P.raw("dve", lambda e: None, waits=[tdb_])
                P.raw("pool", lambda e: None, waits=[tdb_])
                P.raw("act", lambda e: None, waits=[tdb_])
            kys, ysb, fys = ystage.next()
            ys_done = []
            last_pe = None
            for t in range(4):
                tsl = slice(t * 128, (t + 1) * 128)
                for h in range(2):
                    kbs, bs, fbs = banks.next()
                    for dc in range(2):
                        t_s = P.op("pe", lambda e, dc=dc, bs=bs, h=h, tsl=tsl: e.matmul(
                            bs[:, 0:128], lhsT=qk[:, 1, h, dc, tsl], rhs=qk[:, 0, h, dc, tsl],
                            start=(dc == 0), stop=(dc == 1)), waits=[fbs, rot_done] if dc == 0 else (), sig=(dc == 1))
                    ksd, sdb, fsd = sd.next()
                    t_sd = P.op("dve", lambda e, bs=bs, sdb=sdb, h=h: e.tensor_tensor(
                        out=sdb[:, :], in0=bs[:, 0:128], in1=decT[:, h, :], op=ALU.mult), waits=[t_s, fsd, t_c])
                    banks.release(kbs, t_sd)
                    kbk, bkz, fbk = banks.next()
                    bkzb = bkz[:, :].bitcast(BF16)
                    for dc in range(2):
                        t_kt = P.op("pe", lambda e, dc=dc, bkzb=bkzb, h=h, tsl=tsl: e.transpose(
                            out=bkzb[:, dc * 128:(dc + 1) * 128], in_=qk[:, 1, h, dc, tsl], identity=ident[:, :]),
                            waits=[fbk] if dc == 0 else (), sig=(dc == 1))
                    kkz, kzb, fkz = kz.next()
                    t_kz = P.op("act", lambda e, bkzb=bkzb, kzb=kzb, h=h: e.mul(out=kzb[:, :], in_=bkzb[:, 0:256], mul=zeta[:, h:h + 1]),
                                waits=[t_kt, fkz, t_c])
                    banks.release(kbk, t_kz)
                    kby, by, fby = banks.next()
                    P.op("pe", lambda e, by=by, sdb=sdb, t=t, h=h: e.matmul(
                        by[:, 0:256], lhsT=sdb[:, :], rhs=vsb[:, t, h * 256:(h + 1) * 256], start=True, stop=False),
                        waits=[fby, t_sd, vg_done], sig=False)
                    for dc in range(2):
                        t_y = P.op("pe", lambda e, dc=dc, by=by, h=h, tsl=tsl: e.matmul(
                            by[:, 0:256], lhsT=qx[:, h, dc, tsl], rhs=Rb[:, h, dc * 256:(dc + 1) * 256],
                            start=False, stop=(dc == 1)), waits=[Rb_ready[h]] if dc == 0 else (), sig=(dc == 1))
                    sd.release(ksd, t_y)
                    kbu, bu, fbu = banks.next()
                    for dc in range(2):
                        t_u = P.op("pe", lambda e, dc=dc, bu=bu, kzb=kzb, t=t, h=h: e.matmul(
                            bu[:, dc * 256:(dc + 1) * 256], lhsT=kzb[:, dc * 128:(dc + 1) * 128],
                            rhs=vsb[:, t, h * 256:(h + 1) * 256], start=True, stop=True),
                            waits=[fbu, t_kz] if dc == 0 else (), sig=(dc == 1))
                    kz.release(kkz, t_u)
                    t_R = P.op("dve", lambda e, bu=bu, h=h: e.scalar_tensor_tensor(
                        out=R32[:, h, :], in0=R32[:, h, :], scalar=cdec[:, h:h + 1], in1=bu[:, :],
                        op0=ALU.mult, op1=ALU.add), waits=[t_u, R32_ready[h]])
                    banks.release(kbu, t_R)
                    R32_ready[h] = t_R
                    t_Rb = P.op("pool", lambda e, h=h: e.tensor_copy(out=Rb[:, h, :], in_=R32[:, h, :]), waits=[t_R, t_y])
                    Rb_ready[h] = t_Rb
                    R32_ready[h] = [t_R, t_Rb]
                    col = (t * 2 + h)
                    t_bs = P.op("dve", lambda e, by=by, col=col: e.bn_stats(out=st6[:, col, :], in_=by[:, 0:256]), waits=[t_y])
                    t_ba = P.op("dve", lambda e, col=col: e.bn_aggr(out=mv[:, col, 0:2], in_=st6[:, col, :]), waits=[t_bs])
                    t_v1 = P.op("dve", lambda e, col=col: e.tensor_scalar(out=mv[:, col, 2:3], in0=mv[:, col, 1:2],
                                                                          scalar1=EPS, scalar2=None, op0=ALU.add), waits=[t_ba])
                    t_v2 = P.op("act", lambda e, col=col: e.sqrt(out=mv[:, col, 3:4], in_=mv[:, col, 2:3]), waits=[t_v1])
                    t_v3 = P.op("dve", lambda e, col=col: e.reciprocal(out=mv[:, col, 2:3], in_=mv[:, col, 3:4]), waits=[t_v2])
                    kyn, ynb, fyn = yn.next()
                    t_yn = P.op("dve", lambda e, by=by, ynb=ynb, col=col: e.tensor_scalar(
                        out=ynb[:, :], in0=by[:, 0:256], scalar1=mv[:, col, 0:1], scalar2=mv[:, col, 2:3],
                        op0=ALU.subtract, op1=ALU.mult), waits=[t_v3, fyn])
                    banks.release(kby, t_yn)
                    kyo, yob, fyo = yo.next()
                    t_yo = P.op("pool", lambda e, ynb=ynb, yob=yob, t=t, h=h: e.tensor_tensor(
                        out=yob[:, :], in0=ynb[:, :], in1=gsg[:, t, h * 256:(h + 1) * 256], op=ALU.mult),
                        waits=[t_yn, fyo, vg_done])
                    yn.release(kyn, t_yo)
                    kbt, bt, fbt = banks.next()
                    btb = bt[:, :].bitcast(BF16)
                    for ec in range(2):
                        t_t = P.op("pe", lambda e, ec=ec, btb=btb, yob=yob: e.transpose(
                            out=btb[:, ec * 128:(ec + 1) * 128], in_=yob[:, ec * 128:(ec + 1) * 128], identity=ident[:, :]),
                            waits=[fbt, t_yo] if ec == 0 else (), sig=(ec == 1))
                    last_pe = t_t
                    yo.release(kyo, t_t)
                    t_cp = P.op("act", lambda e, btb=btb, ysb=ysb, h=h, tsl=tsl: e.copy(
                        out=ysb[:, h * 2:(h + 1) * 2, tsl], in_=btb[:, 0:256].rearrange("p (e c) -> p e c", e=2)),
                        waits=[t_t, fys])
                    banks.release(kbt, t_cp)
                    ys_done.append(t_cp)
            qk_free[0] = last_pe
            half, col0 = T0 // NTOK, T0 % NTOK
            t_st = None
            for r4 in range(4):
                t_st = P.dma("sp", yv[half * 1024 + r4 * 128: half * 1024 + (r4 + 1) * 128, col0:col0 + BT],
                             ysb[:, r4, :], s_ys[kys], waits=[ys_done] if r4 == 0 else ())
            ystage.release(kys, t_st)
        P.raw("sp", lambda e: None, waits=[ystage.free[0], ystage.free[1]])
        P.emit()


def mixer_host_inputs(g, w_in0, gn_gain, pe_k, w1_k, w2_k, pe_v, w1_v, w2_v, rel_bias):
    m = ret_host_inputs(g, w_in0, gn_gain)
    m.update(nsa_host_inputs(g, w_in0, pe_k, w1_k, w2_k, pe_v, w1_v, w2_v, rel_bias))
    return m


NW = 1292
O_KC, O_VC, O_KS, O_KW, O_VS, O_VW, O_GT = 512, 640, 768, 896, 1024, 1152, 1280
NEGB = -1.0e30


def t5_bucket_np(rel):
    n = np.maximum(rel, 0)
    nf = np.maximum(n, 1).astype(np.float32)
    large = 16 + (np.log(nf / np.float32(16)) / np.float32(math.log(128 / 16)) * np.float32(16)).astype(np.int32)
    large = np.minimum(large, 31)
    return np.where(n < 16, n, large)


def nsa_declare(nc, inp, m):
    m["w_nsa"] = inp("w_nsa", [D, NW]).ap()
    for nm in ("k", "v"):
        m["peT_" + nm] = inp("peT_" + nm, [128, 32]).ap()
        m["w1_" + nm] = inp("w1_" + nm, [128, 32, 128]).ap()
        m["w2_" + nm] = inp("w2_" + nm, [128, 128]).ap()
    m["rb_bc"] = inp("rb_bc", [128, 4, 32]).ap()
    m["bk_c"] = inp("bk_c", [128, 16]).ap()
    m["bk_0"] = inp("bk_0", [128, 128]).ap()
    m["bk_1"] = inp("bk_1", [128, 128]).ap()
    m["wm4"] = inp("wm4", [128, 128], BF16).ap()
    m["E_all"] = inp("E_all", [64, 32, 128], BF16).ap()
    m["selC"] = inp("selC", [32, 128, 64]).ap()
    m["selF"] = inp("selF", [32, 128, 64]).ap()
    m["ovl"] = inp("ovl", [128, 2, 64], BF16).ap()


def nsa_host_inputs(g, w_in0, pe_k, w1_k, w2_k, pe_v, w1_v, w2_v, rel_bias):
    import ml_dtypes
    bf = ml_dtypes.bfloat16
    cols = [np.arange(4096 + g * 512, 4096 + (g + 1) * 512)]
    for base in (5120, 5376, 5632, 6144, 5888, 6400):
        cols.append(np.arange(base + g * 128, base + (g + 1) * 128))
    cols.append(np.arange(6656 + 12 * g, 6656 + 12 * (g + 1)))
    cols = np.concatenate(cols)
    m = {"w_nsa": np.ascontiguousarray(w_in0[:, cols])}
    for nm, pe, w1, w2 in (("k", pe_k, w1_k, w2_k), ("v", pe_v, w1_v, w2_v)):
        m["peT_" + nm] = np.ascontiguousarray(pe.T)
        m["w1_" + nm] = np.ascontiguousarray(np.transpose(w1, (1, 0, 2)))
        m["w2_" + nm] = np.ascontiguousarray(w2)
    m["rb_bc"] = np.ascontiguousarray(np.broadcast_to(rel_bias[4 * g:4 * g + 4][None], (128, 4, 32)))
    i = np.arange(128)
    r = np.arange(-9, 7)
    d = i[:, None] - 16 * r[None, :] - 31
    m["bk_c"] = np.where(d >= 0, t5_bucket_np(d), -1).astype(np.float32)
    t = np.arange(128)[:, None]
    s = np.arange(128)[None, :]
    d0 = s - t
    m["bk_0"] = np.where(d0 >= 0, t5_bucket_np(d0), -1).astype(np.float32)
    m["bk_1"] = t5_bucket_np(128 + s - t).astype(np.float32)
    m["wm4"] = np.where(s >= t, -30000.0, 0.0).astype(np.float32).astype(bf)
    E = np.zeros((64, 32, 128), np.float32)
    for kt in range(32):
        for tl in range(128):
            E[2 * kt + tl // 64, kt, tl] = 1.0
    m["E_all"] = E.astype(bf)
    pos = np.arange(SEQ)
    cur = pos // 64
    blk = np.arange(64)
    causal = (blk[None, :] * 64) <= pos[:, None]
    forced = (blk[None, :] == 0) | (blk[None, :] == cur[:, None]) | (blk[None, :] == cur[:, None] - 1)
    selC = (causal & ~forced).astype(np.float32)
    selF = np.where(forced, 1.0e4 + blk[None, :].astype(np.float32), np.where(causal, 0.0, NEGB)).astype(np.float32)
    m["selC"] = np.ascontiguousarray(selC.reshape(32, 128, 64))
    m["selF"] = np.ascontiguousarray(selF.reshape(32, 128, 64))
    n_cmp = 255
    cmp_idx = np.arange(n_cmp)[:, None] * 16 + np.arange(32)[None, :]
    sel_of = cmp_idx // 64
    overlap = (sel_of[:, :, None] == np.arange(64)[None, None, :]).sum(1).astype(np.float32) / 32.0
    ov = np.zeros((256, 64), np.float32)
    ov[:255] = overlap
    m["ovl"] = np.ascontiguousarray(ov.reshape(2, 128, 64).transpose(1, 0, 2)).astype(bf)
    return m


def nsa_phase(nc, tag, C, M, hT_all_t, yT_in_t, yT_all_t, do_ag=True, nqt=32, branches=(0, 1)):
    BT = 256
    SCALE = 128.0 ** -0.5
    with contextlib.ExitStack() as es:
        P = Prog(nc, es, tag)
        w = sb(nc, es, tag + "w", [128, NCH, NW], BF16)
        hc = Ring([sb(nc, es, tag + "hc%d" % i, [128, NCH, BT], BF16) for i in range(2)])
        s_hc = [P.slot_sem() for _ in range(2)]
        qT = sb(nc, es, tag + "qT", [128, 4, SEQ], BF16)
        kvT = sb(nc, es, tag + "kvT", [128, 4, SEQ], BF16)
        vaug = sb(nc, es, tag + "vaug", [128, 2, 32, 130], BF16)
        sig = sb(nc, es, tag + "sig", [128, 32, 12], F32)
        ident = sb(nc, es, tag + "ident", [128, 128], BF16)
        w1 = [sb(nc, es, tag + "w1%d" % i, [128, 32, 128], BF16) for i in range(2)]
        w2 = [sb(nc, es, tag + "w2%d" % i, [128, 128], BF16) for i in range(2)]
        peT = [sb(nc, es, tag + "peT%d" % i, [128, 32], BF16) for i in range(2)]
        cvec = sb(nc, es, tag + "cvec", [128, 2], F32)
        hdn = [sb(nc, es, tag + "hdn%d" % i, [128, 256], BF16) for i in range(2)]
        kcmpT = sb(nc, es, tag + "kcmpT", [128, 256], BF16)
        vcmp = sb(nc, es, tag + "vcmp", [128, 2, 128], BF16)
        rb = sb(nc, es, tag + "rb", [128, 4, 32], F32)
        rbd = sb(nc, es, tag + "rbd", [128, 4, 32], F32)
        bkc = sb(nc, es, tag + "bkc", [128, 16], F32)
        bk0 = sb(nc, es, tag + "bk0", [128, 128], F32)
        bk1 = sb(nc, es, tag + "bk1", [128, 128], F32)
        Bc = sb(nc, es, tag + "Bc", [128, 4, 16], F32)
        Bf = sb(nc, es, tag + "Bf", [128, 2, 4, 128], F32)
        Bt1 = sb(nc, es, tag + "Bt1", [128, 128], F32)
        Bt2 = sb(nc, es, tag + "Bt2", [128, 128], F32)
        Bhi = sb(nc, es, tag + "Bhi", [128, 2, 4, 128], BF16)
        Blo = sb(nc, es, tag + "Blo", [128, 2, 4, 128], BF16)
        wm4 = sb(nc, es, tag + "wm4", [128, 128], BF16)
        E_all = sb(nc, es, tag + "E", [64, 32, 128], BF16)
        ovl = sb(nc, es, tag + "ovl", [128, 2, 64], BF16)
        selC = Ring([sb(nc, es, tag + "selC%d" % i, [128, 2, 64], F32) for i in range(2)])
        s_sel = [P.slot_sem() for _ in range(2)]
        pbuf = Ring([sb(nc, es, tag + "p%d" % i, [128, 256], F32) for i in range(2)])
        pn = Ring([sb(nc, es, tag + "pn%d" % i, [128, 256], BF16) for i in range(2)])
        pT = Ring([sb(nc, es, tag + "pT%d" % i, [128, 2, 128], BF16) for i in range(2)])
        rs = sb(nc, es, tag + "rs", [128, 16], F32)
        sc = sb(nc, es, tag + "sc", [128, 2, 64], F32)
        m8 = sb(nc, es, tag + "m8", [128, 3, 8], F32)
        negsel = sb(nc, es, tag + "negsel", [128, 64], BF16)
        negselT = Ring([sb(nc, es, tag + "nsT%d" % i, [64, 128], BF16) for i in range(2)])
        PT = Ring([sb(nc, es, tag + "PT%d" % i, [128, 128], BF16) for i in range(4)])
        acc = Ring([sb(nc, es, tag + "acc%d" % i, [128, 512], F32) for i in range(2)])
        accb = Ring([sb(nc, es, tag + "accb%d" % i, [128, 512], BF16) for i in range(2)])
        coef = sb(nc, es, tag + "coef", [128, 16], F32)
        ostage = Ring([sb(nc, es, tag + "os%d" % i, [128, 4, 512], BF16) for i in range(2)])
        s_os = [P.slot_sem() for _ in range(2)]
        banks = Ring([ps(nc, es, tag + "ps%d" % i, [128, 512]) for i in range(5)])
        obanks = Ring([ps(nc, es, tag + "pso%d" % i, [128, 512]) for i in range(2)])
        impbank = ps(nc, es, tag + "psimp", [128, 512])
        imp_free = [None]

        wst = Ring([sb(nc, es, tag + "wst%d" % i, [128, 1, NW], F32) for i in range(1)])
        s_wst = [P.slot_sem() for _ in range(1)]
        wv = M["w_nsa"].rearrange("(c p) n -> p c n", p=128)
        wcast = []
        for c2 in range(16):
            kws, wsb, fws = wst.next()
            tl = P.dma("sp", wsb[:, :, :], wv[:, c2:c2 + 1, :], s_wst[kws], waits=[fws])
            tcw = P.op("dve", lambda e, wsb=wsb, c2=c2: e.tensor_copy(out=w[:, c2:c2 + 1, :], in_=wsb[:, :, :]), waits=[tl])
            wst.release(kws, tcw)
            wcast.append(tcw)
        t_w = wcast
        w1cast = []
        for i, nm in enumerate(("k", "v")):
            for hf in range(4):
                kws, wsb, fws = wst.next()
                flat = wsb[:, :, :].rearrange("p a n -> p (a n)")
                tl = P.dma("sp", flat[:, 0:1024].rearrange("p (l f) -> p l f", l=8), M["w1_" + nm][:, hf * 8:(hf + 1) * 8, :],
                           s_wst[kws], waits=[fws])
                tcw = P.op("dve", lambda e, flat=flat, i=i, hf=hf: e.tensor_copy(
                    out=w1[i][:, hf * 8:(hf + 1) * 8, :], in_=flat[:, 0:1024].rearrange("p (l f) -> p l f", l=8)), waits=[tl])
                wst.release(kws, tcw)
                w1cast.append(tcw)
            kws, wsb, fws = wst.next()
            flat = wsb[:, :, :].rearrange("p a n -> p (a n)")
            P.dma("sp", flat[:, 0:128], M["w2_" + nm], s_wst[kws], waits=[fws])
            tl = P.dma("sp", flat[:, 128:160], M["peT_" + nm], s_wst[kws])
            tcw = P.op("dve", lambda e, flat=flat, i=i: e.tensor_copy(out=w2[i][:, :], in_=flat[:, 0:128]), waits=[tl])
            tcw = P.op("dve", lambda e, flat=flat, i=i: e.tensor_copy(out=peT[i][:, :], in_=flat[:, 128:160]), waits=[tl])
            wst.release(kws, tcw)
            w1cast.append(tcw)
        t_w1 = w1cast
        s_c = P.slot_sem()
        P.dma("sp", ident[:, :], C["ident"], s_c)
        P.dma("sp", rb[:, :, :], M["rb_bc"], s_c)
        P.dma("sp", bkc[:, :], M["bk_c"], s_c)
        P.dma("sp", bk0[:, :], M["bk_0"], s_c)
        P.dma("sp", bk1[:, :], M["bk_1"], s_c)
        P.dma("sp", wm4[:, :], M["wm4"], s_c)
        P.dma("sp", E_all[:, :, :], M["E_all"], s_c)
        t_c = P.dma("sp", ovl[:, :, :], M["ovl"], s_c)
        t_ones = P.op("pool", lambda e: e.memset(vaug[:, :, :, 128:130], 1.0))
        t_z = None
        for b_ in (pn.bufs[0], pn.bufs[1], hdn[0], hdn[1], kcmpT):
            t_z = P.op("pool", lambda e, b_=b_: e.memset(b_[:, :], 0.0))
        t_z = P.op("pool", lambda e: e.memset(vcmp[:, :, :], 0.0))
        tb = None
        for h in range(4):
            tb = P.op("pool", lambda e, h=h: e.tensor_scalar(out=rbd[:, h, :], in0=rb[:, h, :], scalar1=rb[:, h, 31:32],
                                                              scalar2=None, op0=ALU.subtract), waits=[t_c])
        t_bias = None
        for h in range(4):
            for (bk, dst, hasmask) in ((bkc[:, :], Bc[:, h, :], True), (bk0[:, :], Bf[:, 0, h, :], True),
                                       (bk1[:, :], Bf[:, 1, h, :], False)):
                n_ = 16 if bk is not None and dst.shape[-1] == 16 else 128
                t1v = Bt1[:, 0:n_]
                tb = P.op("pool", lambda e, bk=bk, dst=dst: e.tensor_scalar(
                    out=dst, in0=bk, scalar1=-1.0, scalar2=NEGB, op0=ALU.is_equal, op1=ALU.mult), waits=[tb])
                for b in range(31):
                    tb = P.op("pool", lambda e, bk=bk, t1v=t1v, b=b, h=h: e.tensor_scalar(
                        out=t1v, in0=bk, scalar1=float(b), scalar2=rbd[:, h, b:b + 1], op0=ALU.is_equal, op1=ALU.mult),
                        waits=[tb])
                    tb = P.op("pool", lambda e, dst=dst, t1v=t1v: e.tensor_tensor(out=dst, in0=dst, in1=t1v, op=ALU.add),
                              waits=[tb])
            for dl in range(2):
                tb = P.op("pool", lambda e, dl=dl, h=h: e.tensor_copy(out=Bhi[:, dl, h, :], in_=Bf[:, dl, h, :]), waits=[tb])
                tb = P.op("pool", lambda e, dl=dl, h=h: e.tensor_copy(out=Bt2[:, :], in_=Bhi[:, dl, h, :]), waits=[tb])
                tb = P.op("pool", lambda e, dl=dl, h=h: e.tensor_tensor(out=Bt2[:, :], in0=Bf[:, dl, h, :], in1=Bt2[:, :],
                                                                          op=ALU.subtract), waits=[tb])
                tb = P.op("pool", lambda e, dl=dl, h=h: e.tensor_copy(out=Blo[:, dl, h, :], in_=Bt2[:, :]), waits=[tb])
        t_bias = tb

        hv = hT_all_t.ap()
        NBLK = SEQ // BT

        def load_blk(b):
            T0 = b * BT
            r, col0 = T0 // NTOK, T0 % NTOK
            k, buf, fr = hc.next()
            t1 = load_hT_block(P, "sp", buf, hv, r, col0, BT, s_hc[k], [fr])
            return k, buf, t1
        nxt = load_blk(0)
        proj_done = []
        for b in range(NBLK):
            k, hcb, t_h = nxt
            if b + 1 < NBLK:
                nxt = load_blk(b + 1)
            T0 = b * BT
            last = None
            for gi in range(8):
                col0 = gi * 128
                kb, bank, fb = banks.next()
                for c in range(NCH):
                    tm = P.op("pe", lambda e, c=c, bank=bank, col0=col0, hcb=hcb: e.matmul(
                        bank[:, 0:BT], lhsT=w[:, c, col0:col0 + 128], rhs=hcb[:, c, :],
                        start=(c == 0), stop=(c == NCH - 1)), waits=[t_w, t_h, fb] if c == 0 else (), sig=(c == NCH - 1))
                if gi < 4:
                    te = P.op("act", lambda e, bank=bank, gi=gi, T0=T0: e.mul(out=qT[:, gi, T0:T0 + BT], in_=bank[:, 0:BT], mul=SCALE),
                              waits=[tm])
                else:
                    te = P.op("dve", lambda e, bank=bank, gi=gi, T0=T0: e.tensor_copy(out=kvT[:, gi - 4, T0:T0 + BT], in_=bank[:, 0:BT]),
                              waits=[tm])
                banks.release(kb, te)
                proj_done.append(te)
            for t in range(BT // 128):
                tile_i = (T0 // 128) + t
                kb, bank, fb = banks.next()
                for c in range(NCH):
                    tm = P.op("pe", lambda e, c=c, bank=bank, t=t, hcb=hcb: e.matmul(
                        bank[:, 0:268], lhsT=hcb[:, c, t * 128:(t + 1) * 128], rhs=w[:, c, O_VS:NW],
                        start=(c == 0), stop=(c == NCH - 1)), waits=[fb] if c == 0 else (), sig=(c == NCH - 1))
                last = tm
                ta = P.op("dve", lambda e, bank=bank, tile_i=tile_i: e.tensor_copy(
                    out=vaug[:, :, tile_i, 0:128], in_=bank[:, 0:256].rearrange("p (a d) -> p a d", a=2)), waits=[tm])
                tg = P.op("act", lambda e, bank=bank, tile_i=tile_i: e.activation(
                    out=sig[:, tile_i, :], in_=bank[:, 256:268], func=AF.Sigmoid), waits=[tm])
                banks.release(kb, [ta, tg])
                proj_done += [ta, tg]
            hc.release(k, last)

        cmp_done = []
        for i in range(2):
            src = kvT[:, i, :]
            kb, bank, fb = banks.next()
            for l in range(32):
                tm = P.op("pe", lambda e, l=l, bank=bank, i=i: e.matmul(
                    bank[:, 0:1], lhsT=w1[i][:, l, :], rhs=peT[i][:, l:l + 1], start=(l == 0), stop=(l == 31)),
                    waits=[fb, t_w1] if l == 0 else (), sig=(l == 31))
            tcv = P.op("dve", lambda e, bank=bank, i=i: e.tensor_copy(out=cvec[:, i:i + 1], in_=bank[:, 0:1]), waits=[tm])
            banks.release(kb, tcv)
            kb, bank, fb = banks.next()
            for l in range(32):
                tm = P.op("pe", lambda e, l=l, bank=bank, i=i, src=src: e.matmul(
                    bank[:, 0:255], lhsT=w1[i][:, l, :], rhs=src[:, l:l + 16 * 254 + 1:16], start=(l == 0), stop=(l == 31)),
                    waits=[fb, proj_done] if l == 0 else (), sig=(l == 31))
            th = P.op("act", lambda e, bank=bank, i=i: e.activation(out=hdn[i][:, 0:255], in_=bank[:, 0:255], func=AF.Silu,
                                                                    bias=cvec[:, i:i + 1]), waits=[tm, tcv, t_z])
            banks.release(kb, th)
            if i == 0:
                kb, bank, fb = banks.next()
                tm = P.op("pe", lambda e, bank=bank: e.matmul(bank[:, 0:255], lhsT=w2[0][:, :], rhs=hdn[0][:, 0:255],
                                                              start=True, stop=True), waits=[fb, th])
                tk = P.op("dve", lambda e, bank=bank: e.tensor_copy(out=kcmpT[:, 0:255], in_=bank[:, 0:255]), waits=[tm, t_z])
                banks.release(kb, tk)
                cmp_done.append(tk)
            else:
                for nt, cnt in ((0, 128), (1, 127)):
                    kb, bank, fb = banks.next()
                    tm = P.op("pe", lambda e, bank=bank, nt=nt, cnt=cnt: e.matmul(
                        bank[0:cnt, 0:128], lhsT=hdn[1][:, nt * 128:nt * 128 + cnt], rhs=w2[1][:, :], start=True, stop=True),
                        waits=[fb, th])
                    tk = P.op("dve", lambda e, bank=bank, nt=nt, cnt=cnt: e.tensor_copy(out=vcmp[0:cnt, nt, :], in_=bank[0:cnt, 0:128]),
                              waits=[tm, t_z])
                    banks.release(kb, tk)
                    cmp_done.append(tk)

        yv = yT_in_t.ap()
        ready = [proj_done, cmp_done, t_bias, t_ones]
        kos = osb = fos = None
        os_done = []
        for qt in range(nqt):
            q0 = qt * 128
            qs = slice(q0, q0 + 128)
            ksl, slb, fsl = selC.next()
            P.dma("sp", slb[:, 0, :], M["selC"][qt], s_sel[ksl], waits=[fsl])
            t_sl = P.dma("sp", slb[:, 1, :], M["selF"][qt], s_sel[ksl])
            kac, accq, fac = acc.next()
            W = min(255, q0 // 16 + 7)
            nlo = max(0, q0 // 16 - 9)
            c0 = nlo - q0 // 16 + 9
            nts = 2 if W > 128 else 1
            bimp, fbi = impbank, imp_free[0]
            t_imp = None
            acc_tok = [None] * 4
            for h in range(4):
                kb, bL, fb = banks.next()
                tm = P.op("pe", lambda e, bL=bL, h=h, qs=qs, W=W: e.matmul(
                    bL[:, 0:W], lhsT=qT[:, h, qs], rhs=kcmpT[:, 0:W], start=True, stop=True), waits=[fb, ready])
                tbd = P.op("dve", lambda e, bL=bL, h=h, nlo=nlo, W=W, c0=c0: e.tensor_tensor(
                    out=bL[:, nlo:W], in0=bL[:, nlo:W], in1=Bc[:, h, c0:c0 + (W - nlo)], op=ALU.add), waits=[tm, t_bias])
                kp, pb, fp = pbuf.next()
                col = (qt % 2) * 8 + h
                te = P.op("act", lambda e, bL=bL, pb=pb, W=W, col=col: e.activation(
                    out=pb[:, 0:W], in_=bL[:, 0:W], func=AF.Exp, accum_out=rs[:, col:col + 1]), waits=[tbd, fp])
                banks.release(kb, te)
                t1_ = P.op("dve", lambda e, col=col: e.tensor_scalar(out=rs[:, col:col + 1], in0=rs[:, col:col + 1],
                                                                     scalar1=1e-30, scalar2=None, op0=ALU.max), waits=[te])
                t2_ = P.op("dve", lambda e, col=col: e.reciprocal(out=rs[:, col:col + 1], in_=rs[:, col:col + 1]), waits=[t1_])
                kn, pnb, fn_ = pn.next()
                t3_ = P.op("dve", lambda e, pb=pb, pnb=pnb, W=W, col=col: e.tensor_scalar(
                    out=pnb[:, 0:W], in0=pb[:, 0:W], scalar1=rs[:, col:col + 1], scalar2=None, op0=ALU.mult),
                    waits=[t2_, fn_, t_z])
                pbuf.release(kp, t3_)
                kb, bT, fb = banks.next()
                bTb = bT[:, :].bitcast(BF16)
                for nt in range(nts):
                    tt_ = P.op("pe", lambda e, nt=nt, bTb=bTb, pnb=pnb: e.transpose(
                        out=bTb[:, nt * 128:(nt + 1) * 128], in_=pnb[:, nt * 128:(nt + 1) * 128], identity=ident[:, :]),
                        waits=[fb, t3_] if nt == 0 else (), sig=(nt == nts - 1))
                pn.release(kn, tt_)
                kpt, pTb, fpt = pT.next()
                tcp = P.op("act", lambda e, bTb=bTb, pTb=pTb, nts=nts: e.copy(
                    out=pTb[:, 0:nts, :], in_=bTb[:, 0:nts * 128].rearrange("p (a s) -> p a s", a=nts)), waits=[tt_, fpt])
                banks.release(kb, tcp)
                kb, bO, fb = obanks.next()
                for nt in range(nts):
                    to_ = P.op("pe", lambda e, nt=nt, bO=bO, pTb=pTb: e.matmul(
                        bO[:, 0:128], lhsT=pTb[:, nt, :], rhs=vcmp[:, nt, :], start=(nt == 0), stop=(nt == nts - 1)),
                        waits=[fb, tcp] if nt == 0 else (), sig=(nt == nts - 1))
                for nt in range(nts):
                    t_imp = P.op("pe", lambda e, nt=nt, bimp=bimp, pTb=pTb, h=h: e.matmul(
                        bimp[:, 0:64], lhsT=pTb[:, nt, :], rhs=ovl[:, nt, :],
                        start=(h == 0 and nt == 0), stop=(h == 3 and nt == nts - 1)),
                        waits=[fbi] if (h == 0 and nt == 0) else (), sig=(nt == nts - 1))
                pT.release(kpt, t_imp)
                ta_ = P.op("dve", lambda e, bO=bO, accq=accq, h=h, qt=qt: e.tensor_scalar(
                    out=accq[:, h * 128:(h + 1) * 128], in0=bO[:, 0:128], scalar1=sig[:, qt, h * 3:h * 3 + 1],
                    scalar2=None, op0=ALU.mult), waits=[to_, fac])
                obanks.release(kb, ta_)
                acc_tok[h] = ta_
            ts1 = P.op("dve", lambda e, bimp=bimp, slb=slb: e.tensor_tensor(out=sc[:, 0, :], in0=bimp[:, 0:64], in1=slb[:, 0, :],
                                                                            op=ALU.mult), waits=[t_imp, t_sl])
            imp_free[0] = ts1
            ts2 = P.op("dve", lambda e, slb=slb: e.tensor_tensor(out=sc[:, 0, :], in0=sc[:, 0, :], in1=slb[:, 1, :], op=ALU.add),
                       waits=[ts1])
            selC.release(ksl, ts2)
            ts3 = P.op("dve", lambda e: e.max(out=m8[:, 0, :], in_=sc[:, 0, :]), waits=[ts2])
            ts4 = P.op("dve", lambda e: e.match_replace(out=sc[:, 1, :], in_to_replace=m8[:, 0, :], in_values=sc[:, 0, :],
                                                        imm_value=NEGB), waits=[ts3])
            ts5 = P.op("dve", lambda e: e.max(out=m8[:, 1, :], in_=sc[:, 1, :]), waits=[ts4])
            ts6 = P.op("dve", lambda e: e.tensor_scalar(out=m8[:, 2, 0:1], in0=m8[:, 1, 7:8], scalar1=-5e29, scalar2=None,
                                                        op0=ALU.max), waits=[ts5])
            ts7 = P.op("dve", lambda e: e.tensor_scalar(out=negsel[:, :], in0=sc[:, 0, :], scalar1=m8[:, 2, 0:1], scalar2=-30000.0,
                                                        op0=ALU.is_lt, op1=ALU.mult), waits=[ts6, getattr(P, "negsel_free", None)])
            kb, bT, fb = banks.next()
            bTb = bT[:, :].bitcast(BF16)
            tt_ = P.op("pe", lambda e, bTb=bTb: e.transpose(out=bTb[0:64, 0:128], in_=negsel[:, 0:64], identity=ident[:, :]),
                       waits=[fb, ts7])
            P.negsel_free = tt_
            kns, nsT, fns = negselT.next()
            tns = P.op("act", lambda e, bTb=bTb, nsT=nsT: e.copy(out=nsT[:, :], in_=bTb[0:64, 0:128]), waits=[tt_, fns])
            banks.release(kb, tns)
            last_sel_pe = None
            for br in branches:
                kts = list(range(0, qt + 1)) if br == 0 else list(range(max(0, qt - 4), qt + 1))
                ksrc = 2 if br == 0 else 3
                for h in range(4):
                    kbo, bO, fbo = obanks.next()
                    for ii, kt in enumerate(kts):
                        dl = qt - kt
                        ksl_ = slice(kt * 128, (kt + 1) * 128)
                        kb, bL, fb = banks.next()
                        extra = []
                        if br == 0:
                            extra.append((E_all[:, kt, :], nsT[:, :]))
                        if dl <= 1:
                            extra.append((ident[:, :], Bhi[:, dl, h, :]))
                            extra.append((ident[:, :], Blo[:, dl, h, :]))
                        if br == 1 and dl == 4:
                            extra.append((ident[:, :], wm4[:, :]))
                        tm = P.op("pe", lambda e, bL=bL, ksrc=ksrc, ksl_=ksl_, h=h, qs=qs, extra=extra: e.matmul(
                            bL[:, 0:128], lhsT=kvT[:, ksrc, ksl_], rhs=qT[:, h, qs], start=True, stop=(len(extra) == 0)),
                            waits=[fb, tns if br == 0 else None], sig=(len(extra) == 0))
                        for xi_, (l_, r_) in enumerate(extra):
                            tm = P.op("pe", lambda e, bL=bL, l_=l_, r_=r_, xi_=xi_, extra=extra: e.matmul(
                                bL[:, 0:128], lhsT=l_, rhs=r_, start=False, stop=(xi_ == len(extra) - 1)),
                                sig=(xi_ == len(extra) - 1))
                        kpt, ptb, fpt = PT.next()
                        te = P.op("act", lambda e, bL=bL, ptb=ptb: e.activation(out=ptb[:, :], in_=bL[:, 0:128], func=AF.Exp),
                                  waits=[tm, fpt])
                        banks.release(kb, te)
                        to_ = P.op("pe", lambda e, bO=bO, ptb=ptb, br=br, kt=kt, ii=ii, kts=kts: e.matmul(
                            bO[:, 0:129], lhsT=ptb[:, :], rhs=vaug[:, br, kt, 0:129], start=(ii == 0), stop=(ii == len(kts) - 1)),
                            waits=[te, fbo] if ii == 0 else [te], sig=True)
                        PT.release(kpt, to_)
                    last_sel_pe = to_
                    col = h * 2 + br
                    tc1 = P.op("dve", lambda e, bO=bO, col=col: e.reciprocal(out=coef[:, col:col + 1], in_=bO[:, 128:129]), waits=[to_])
                    tc2 = P.op("dve", lambda e, col=col, qt=qt, h=h, br=br: e.tensor_tensor(
                        out=coef[:, col:col + 1], in0=coef[:, col:col + 1], in1=sig[:, qt, h * 3 + 1 + br:h * 3 + 2 + br], op=ALU.mult),
                        waits=[tc1])
                    tc3 = P.op("dve", lambda e, bO=bO, accq=accq, h=h, col=col: e.scalar_tensor_tensor(
                        out=accq[:, h * 128:(h + 1) * 128], in0=bO[:, 0:128], scalar=coef[:, col:col + 1],
                        in1=accq[:, h * 128:(h + 1) * 128], op0=ALU.mult, op1=ALU.add), waits=[tc2, acc_tok[h]])
                    acc_tok[h] = tc3
                    obanks.release(kbo, tc3)
            negselT.release(kns, last_sel_pe)
            kab, ab, fab = accb.next()
            tcb = P.op("act", lambda e, ab=ab, accq=accq: e.copy(out=ab[:, :], in_=accq[:, :]), waits=[acc_tok, fab])
            acc.release(kac, tcb)
            kb, bT, fb = banks.next()
            bTb = bT[:, :].bitcast(BF16)
            for h in range(4):
                tt_ = P.op("pe", lambda e, h=h, bTb=bTb, ab=ab: e.transpose(
                    out=bTb[:, h * 128:(h + 1) * 128], in_=ab[:, h * 128:(h + 1) * 128], identity=ident[:, :]),
                    waits=[fb, tcb] if h == 0 else (), sig=(h == 3))
            accb.release(kab, tt_)
            if qt % 4 == 0:
                kos, osb, fos = ostage.next()
                os_done = []
            tcp = P.op("act", lambda e, bTb=bTb, osb=osb, qt=qt: e.copy(
                out=osb[:, :, (qt % 4) * 128:(qt % 4 + 1) * 128], in_=bTb[:, 0:512].rearrange("p (h s) -> p h s", h=4)),
                waits=[tt_, fos])
            banks.release(kb, tcp)
            os_done.append(tcp)
            if qt % 4 == 3:
                T0 = (qt - 3) * 128
                half, col0 = T0 // NTOK, T0 % NTOK
                t_st = None
                for h in range(4):
                    t_st = P.dma("sp", yv[half * 1024 + 512 + h * 128: half * 1024 + 512 + (h + 1) * 128, col0:col0 + 512],
                                 osb[:, h, :], s_os[kos], waits=[os_done] if h == 0 else ())
                ostage.release(kos, t_st)
        if do_ag:
            t_ag = coll_allgather(P, yT_in_t, yT_all_t, [ostage.free[0], ostage.free[1]])
            P.raw("pool", lambda e: None, waits=[t_ag])
        else:
            P.raw("sp", lambda e: None, waits=[ostage.free[0], ostage.free[1]])
        P.emit()
```

```python
import contextlib
import math
import numpy as np
import concourse.bass as bass
import concourse.mybir as mybir
from concourse.bass_utils import run_bass_kernel_spmd

F32 = mybir.dt.float32
BF16 = mybir.dt.bfloat16
ALU = mybir.AluOpType
AF = mybir.ActivationFunctionType
AX = mybir.AxisListType

D = 2048
DFF = 5632
NTOK = 2048
SEQ = 4096
EPS = 1e-6
NCH = D // 128
NFF = DFF // 128


def freeze(fn):
    import types
    if getattr(fn, "__closure__", None) is None:
        return fn
    cells = []
    for c in fn.__closure__:
        try:
            cells.append(types.CellType(c.cell_contents))
        except ValueError:
            cells.append(c)
    return types.FunctionType(fn.__code__, fn.__globals__, fn.__name__, fn.__defaults__, tuple(cells))


_SEM_REG = {}


class Prog:
    ENG = ("pe", "act", "dve", "pool", "sp")

    def __init__(self, nc, es, tag):
        self.nc = nc
        self.es = es
        self.tag = tag
        self.q = {e: [] for e in self.ENG}
        reg = _SEM_REG.setdefault(id(nc), {"es": contextlib.ExitStack(), "sems": {}, "cnt": {}, "nc": nc})
        self.reg = reg
        self.sems = reg["sems"]
        self.cnt = reg["cnt"]
        self.seen = {e: {} for e in self.ENG}
        for e in self.ENG:
            self.newsem("p_" + e)
        self.nsem = 0

    def newsem(self, name):
        if name not in self.sems:
            h = self.reg["es"].enter_context(self.nc.semaphore("g_" + name))
            self.sems[name] = h
            self.cnt[name] = 0
        return name

    def slot_sem(self):
        self.nsem += 1
        return self.newsem("s%d" % self.nsem)

    def _waits(self, eng, waits):
        for tok in waits:
            if tok is None:
                continue
            if isinstance(tok, list):
                self._waits(eng, tok)
                continue
            name, val = tok
            if self.seen[eng].get(name, 0) >= val:
                continue
            self.seen[eng][name] = val
            self.q[eng].append(("w", name, val))

    def op(self, eng, fn, waits=(), sig=True):
        self._waits(eng, waits)
        fn = freeze(fn)
        if sig:
            name = "p_" + eng
            self.cnt[name] += 1
            self.q[eng].append(("i", fn, name, 1))
            return (name, self.cnt[name])
        self.q[eng].append(("i", fn, None, 0))
        return None

    def dma(self, eng, out, in_, sem, waits=(), **kw):
        self._waits(eng, waits)
        self.cnt[sem] += 16
        self.q[eng].append(("i", lambda e: e.dma_start(out=out, in_=in_, **kw), sem, 16))
        return (sem, self.cnt[sem])

    def raw(self, eng, fn, waits=()):
        self._waits(eng, waits)
        self.q[eng].append(("r", freeze(fn)))

    def emit(self):
        nc = self.nc

        def replay(eobj, items):
            for it in items:
                if it[0] == "w":
                    eobj.wait_ge(self.sems[it[1]], it[2])
                elif it[0] == "r":
                    it[1](eobj)
                elif it[0] == "c":
                    it[1](eobj).then_inc(self.sems[it[2]])
                else:
                    ins = it[1](eobj)
                    if it[2] is not None:
                        ins.then_inc(self.sems[it[2]], it[3])

        with nc.Block() as block:
            block.tensor(lambda e: replay(e, self.q["pe"]))
            block.scalar(lambda e: replay(e, self.q["act"]))
            block.vector(lambda e: replay(e, self.q["dve"]))
            block.gpsimd(lambda e: replay(e, self.q["pool"]))
            block.sync(lambda e: replay(e, self.q["sp"]))


class Ring:
    def __init__(self, bufs):
        self.bufs = bufs
        self.n = len(bufs)
        self.i = 0
        self.free = [None] * self.n

    def next(self):
        k = self.i % self.n
        self.i += 1
        return k, self.bufs[k], self.free[k]

    def release(self, k, tok):
        self.free[k] = tok


def sb(nc, es, name, shape, dt):
    return es.enter_context(nc.sbuf_tensor(name, shape, dt))


def ps(nc, es, name, shape, dt=F32):
    return es.enter_context(nc.psum_tensor(name, shape, dt))


def bcast_rows(ap1d_tensor, offset, n):
    return bass.AP(tensor=ap1d_tensor, offset=offset, ap=[[0, 128], [1, n]])


class NormCtx:
    def __init__(self, nc, es, P, tag, gain_ap, ident_ap, psum_banks):
        self.P = P
        self.gain_bc = sb(nc, es, tag + "gain", [128, D], F32)
        self.ident = sb(nc, es, tag + "ident", [128, 128], BF16)
        self.ss = sb(nc, es, tag + "ss", [128, 64], F32)
        self.rstd = sb(nc, es, tag + "rstd", [128, 64], F32)
        self.xn = Ring([sb(nc, es, tag + "xn%d" % i, [128, D], BF16) for i in range(2)])
        self.banks = psum_banks
        s = P.slot_sem()
        P.dma("sp", self.gain_bc[:, :], gain_ap, s)
        self.t_const = P.dma("sp", self.ident[:, :], ident_ap, s)
        self.col = 0

    def tile(self, xt, xt_tok, dst_fn, want_rows=None):
        P = self.P
        col = self.col % 64
        self.col += 1
        ss, rstd, gain_bc, ident = self.ss, self.rstd, self.gain_bc, self.ident
        assert self.col <= 64
        k, xn, fr = self.xn.next()
        t_sq = P.op("act", lambda e: e.activation(out=xn[:, :], in_=xt[:, :], func=AF.Square,
                                                  accum_out=ss[:, col:col + 1]),
                    waits=[xt_tok, fr])
        t_a = P.op("dve", lambda e: e.tensor_scalar(out=rstd[:, col:col + 1], in0=ss[:, col:col + 1],
                                                    scalar1=1.0 / D, scalar2=EPS, op0=ALU.mult, op1=ALU.add),
                   waits=[t_sq])
        t_a2 = P.op("act", lambda e: e.sqrt(out=ss[:, col:col + 1], in_=rstd[:, col:col + 1]), waits=[t_a])
        t_b = P.op("dve", lambda e: e.reciprocal(out=rstd[:, col:col + 1], in_=ss[:, col:col + 1]), waits=[t_a2])
        if want_rows is not None:
            t_n = want_rows(rstd[:, col:col + 1], [t_b, self.t_const])
            return [t_n], t_n
        t_n = P.op("dve", lambda e: e.scalar_tensor_tensor(out=xn[:, :], in0=xt[:, :], scalar=rstd[:, col:col + 1],
                                                           in1=gain_bc[:, :], op0=ALU.mult, op1=ALU.mult),
                   waits=[t_b, fr, self.t_const])
        toks = []
        tk = None
        for half in range(2):
            kp, pT, frp = self.banks.next()
            pTb = pT[:, :].bitcast(BF16)
            for c8 in range(8):
                c = half * 8 + c8
                tk = P.op("pe", lambda e, c=c, c8=c8, pTb=pTb, xn=xn: e.transpose(
                    out=pTb[:, c8 * 128:(c8 + 1) * 128], in_=xn[:, c * 128:(c + 1) * 128], identity=ident[:, :]),
                    waits=[t_n, frp] if c8 == 0 else (), sig=(c8 == 7))
            tc = dst_fn(half, pTb, tk)
            self.banks.release(kp, tc)
            toks.append(tc)
        self.xn.release(k, tk)
        return toks, t_n


def ffn_phase(nc, tag, C, x_src, gain_t, w1_d, w3_d, w2_d, x_dst):
    TG = 1024
    NTT = TG // 128
    WB = 256
    NBLK = DFF // WB
    aT_free_holder = [None]
    with contextlib.ExitStack() as es:
        P = Prog(nc, es, tag)
        hT = sb(nc, es, tag + "hT", [128, NCH, TG], BF16)
        aT = sb(nc, es, tag + "aT", [128, NFF, TG], BF16)
        xin = Ring([sb(nc, es, tag + "xin%d" % i, [128, D], F32) for i in range(2)])
        s_xin = [P.slot_sem() for _ in range(2)]
        wa = Ring([sb(nc, es, tag + "wa%d" % i, [128, NCH, WB], BF16) for i in range(2)])
        wb = Ring([sb(nc, es, tag + "wb%d" % i, [128, NCH, WB], BF16) for i in range(2)])
        s_wa = [P.slot_sem() for _ in range(2)]
        s_wb = [P.slot_sem() for _ in range(2)]
        silu = Ring([sb(nc, es, tag + "silu%d" % i, [128, 512], F32) for i in range(2)])
        w2r = Ring([sb(nc, es, tag + "w2r%d" % i, [128, 4, 512], BF16) for i in range(2)])
        s_w2 = [P.slot_sem() for _ in range(2)]
        xres = Ring([sb(nc, es, tag + "xres%d" % i, [128, 512], F32) for i in range(2)])
        s_xr = [P.slot_sem() for _ in range(2)]
        xo = Ring([sb(nc, es, tag + "xo%d" % i, [128, 512], F32) for i in range(2)])
        s_xo = [P.slot_sem() for _ in range(2)]
        banks = Ring([ps(nc, es, tag + "ps%d" % i, [128, 512]) for i in range(8)])
        NC_ = NormCtx(nc, es, P, tag, bcast_rows(gain_t, 0, D), C["ident"], banks)

        w1v = w1_d.rearrange("(c p) n -> p c n", p=128)
        w3v = w3_d.rearrange("(c p) n -> p c n", p=128)

        for grp in range(NTOK // TG):
            tok0 = grp * TG
            hT_ready = []

            def load_x(tt):
                k, buf, fr = xin.next()
                t = P.dma("sp", buf[:, :], x_src[tok0 + tt * 128: tok0 + (tt + 1) * 128, :], s_xin[k], waits=[fr])
                return k, buf, t

            nxt = load_x(0)
            for tt in range(NTT):
                k, buf, t = nxt
                if tt + 1 < NTT:
                    nxt = load_x(tt + 1)

                def dst(half, pTb, tk, tt=tt):
                    return P.op("act", lambda e: e.copy(
                        out=hT[:, half * 8:(half + 1) * 8, tt * 128:(tt + 1) * 128],
                        in_=pTb.rearrange("p (c t) -> p c t", c=8)), waits=[tk, aT_free_holder[0]])
                toks, t_free = NC_.tile(buf, t, dst)
                xin.release(k, t_free)
                hT_ready += toks
            def load_w(blk):
                ka, a, fa = wa.next()
                ta = P.dma("pool", a[:, :, :], w1v[:, :, blk * WB:(blk + 1) * WB], s_wa[ka], waits=[fa])
                kb, b, fb = wb.next()
                tb = P.dma("pool", b[:, :, :], w3v[:, :, blk * WB:(blk + 1) * WB], s_wb[kb], waits=[fb])
                return (ka, a, ta, kb, b, tb)

            nw = load_w(0)
            aT_ready = []
            for blk in range(NBLK):
                ka, a, ta, kb, b, tb = nw
                if blk + 1 < NBLK:
                    nw = load_w(blk + 1)
                last_pe = None
                for sub in range(WB // 128):
                    j = blk * (WB // 128) + sub
                    for th in range(TG // 512):
                        kA, bA, fA = banks.next()
                        kB, bB, fB = banks.next()
                        for c in range(NCH):
                            tA = P.op("pe", lambda e, c=c, a=a, bA=bA, sub=sub, th=th: e.matmul(
                                bA[:, :], lhsT=a[:, c, sub * 128:(sub + 1) * 128], rhs=hT[:, c, th * 512:(th + 1) * 512],
                                start=(c == 0), stop=(c == NCH - 1)),
                                waits=[ta, fA, hT_ready] if c == 0 else (), sig=(c == NCH - 1))
                        for c in range(NCH):
                            tB = P.op("pe", lambda e, c=c, b=b, bB=bB, sub=sub, th=th: e.matmul(
                                bB[:, :], lhsT=b[:, c, sub * 128:(sub + 1) * 128], rhs=hT[:, c, th * 512:(th + 1) * 512],
                                start=(c == 0), stop=(c == NCH - 1)),
                                waits=[tb, fB] if c == 0 else (), sig=(c == NCH - 1))
                        last_pe = tB
                        ks, sbuf_, fs = silu.next()
                        t_s = P.op("act", lambda e, bA=bA, sbuf_=sbuf_: e.activation(
                            out=sbuf_[:, :], in_=bA[:, :], func=AF.Silu), waits=[tA, fs])
                        banks.release(kA, t_s)
                        t_m = P.op("dve", lambda e, bB=bB, sbuf_=sbuf_, j=j, th=th: e.tensor_tensor(
                            out=aT[:, j, th * 512:(th + 1) * 512], in0=sbuf_[:, :], in1=bB[:, :], op=ALU.mult),
                            waits=[t_s, tB, aT_free_holder[0]])
                        banks.release(kB, t_m)
                        silu.release(ks, t_m)
                        aT_ready.append(t_m)
                wa.release(ka, last_pe)
                wb.release(kb, last_pe)
            last_mm = None
            for n in range(D // 512):
                acc = [banks.next() for _ in range(NTT)]
                for jb in range(NFF // 4):
                    kw, wbuf, fw = w2r.next()
                    tw = P.dma("pool", wbuf[:, :, :],
                               w2_d[jb * 512:(jb + 1) * 512, n * 512:(n + 1) * 512].rearrange("(q p) c -> p q c", p=128),
                               s_w2[kw], waits=[fw])
                    for q in range(4):
                        j = jb * 4 + q
                        for tt in range(NTT):
                            kb_, bk, fb_ = acc[tt]
                            w_ = []
                            if j == 0:
                                w_ = [fb_, aT_ready]
                            if q == 0 and tt == 0:
                                w_ = w_ + [tw]
                            last_mm = P.op("pe", lambda e, bk=bk, j=j, tt=tt, wbuf=wbuf, q=q: e.matmul(
                                bk[:, :], lhsT=aT[:, j, tt * 128:(tt + 1) * 128], rhs=wbuf[:, q, :],
                                start=(j == 0), stop=(j == NFF - 1)),
                                waits=w_, sig=(j == NFF - 1) or (q == 3 and tt == NTT - 1))
                            if j == NFF - 1:
                                acc[tt] = (kb_, bk, last_mm)
                    w2r.release(kw, last_mm)
                for tt in range(NTT):
                    kb_, bk, t_acc = acc[tt]
                    kr, xr, fr = xres.next()
                    rows = slice(tok0 + tt * 128, tok0 + (tt + 1) * 128)
                    t_r = P.dma("sp", xr[:, :], x_src[rows, n * 512:(n + 1) * 512], s_xr[kr], waits=[fr])
                    ko, o, fo = xo.next()
                    t_e = P.op("dve", lambda e, o=o, bk=bk, xr=xr: e.scalar_tensor_tensor(
                        out=o[:, :], in0=bk[:, :], scalar=0.5, in1=xr[:, :], op0=ALU.mult, op1=ALU.add),
                        waits=[t_acc, t_r, fo])
                    banks.release(kb_, t_e)
                    xres.release(kr, t_e)
                    t_st = P.dma("sp", x_dst[rows, n * 512:(n + 1) * 512], o[:, :], s_xo[ko], waits=[t_e])
                    xo.release(ko, t_st)
            aT_free_holder[0] = last_mm
        P.raw("sp", lambda e: None, waits=[xo.free[i] for i in range(2)])
        P.emit()


def coll_allgather(P, src_t, dst_t, waits):
    P._waits("pool", waits)
    toks = []
    for k in range(4):
        nag = P.reg.setdefault("nag", 0)
        P.reg["nag"] = nag + 1
        sem = P.newsem("ag%d" % nag)
        P.cnt[sem] += 1

        def fn(e, k=k):
            return e.collective_compute("AllGather", ALU.bypass,
                                        replica_groups=[[0, 1], [2, 3], [4, 5], [6, 7]],
                                        ins=[src_t.ap()[k * 512:(k + 1) * 512, :]],
                                        outs=[dst_t.ap()[k * 1024:(k + 1) * 1024, :]])
        P.q["pool"].append(("c", fn, sem))
        toks.append((sem, 1))
    return toks


def normT_phase(nc, tag, C, x_src, gain_t, hT_in_t, hT_all_t):
    with contextlib.ExitStack() as es:
        P = Prog(nc, es, tag)
        stage = sb(nc, es, tag + "stage", [128, NCH, NTOK], BF16)
        xin = Ring([sb(nc, es, tag + "xin%d" % i, [128, D], F32) for i in range(2)])
        s_xin = [P.slot_sem() for _ in range(2)]
        banks = Ring([ps(nc, es, tag + "ps%d" % i, [128, 512]) for i in range(4)])
        NC_ = NormCtx(nc, es, P, tag, bcast_rows(gain_t, 0, D), C["ident"], banks)
        NT = NTOK // 128
        cp = []

        def load_x(tt):
            k, buf, fr = xin.next()
            t = P.dma("sp", buf[:, :], x_src[tt * 128:(tt + 1) * 128, :], s_xin[k], waits=[fr])
            return k, buf, t
        nxt = load_x(0)
        for tt in range(NT):
            k, buf, t = nxt
            if tt + 1 < NT:
                nxt = load_x(tt + 1)

            def dst(half, pTb, tk, tt=tt):
                return P.op("act", lambda e: e.copy(
                    out=stage[:, half * 8:(half + 1) * 8, tt * 128:(tt + 1) * 128],
                    in_=pTb.rearrange("p (c t) -> p c t", c=8)), waits=[tk])
            toks, t_free = NC_.tile(buf, t, dst)
            xin.release(k, t_free)
            cp += toks
        s_out = P.slot_sem()
        hv = hT_in_t.ap()
        t_o = None
        for c in range(NCH):
            t_o = P.dma("sp", hv[c * 128:(c + 1) * 128, :], stage[:, c, :], s_out, waits=[cp] if c == 0 else ())
        t_ag = coll_allgather(P, hT_in_t, hT_all_t, [t_o])
        P.raw("pool", lambda e: None, waits=[t_ag])
        P.emit()


def wout_phase(nc, tag, C, yT_all_t, wo_d, x1, x2):
    with contextlib.ExitStack() as es:
        P = Prog(nc, es, tag)
        wo = sb(nc, es, tag + "wo", [128, NCH, D], BF16)
        yT = sb(nc, es, tag + "yT", [128, NCH, NTOK], BF16)
        xres = Ring([sb(nc, es, tag + "xres%d" % i, [128, 512], F32) for i in range(3)])
        s_xr = [P.slot_sem() for _ in range(3)]
        xo = Ring([sb(nc, es, tag + "xo%d" % i, [128, 512], F32) for i in range(3)])
        s_xo = [P.slot_sem() for _ in range(3)]
        banks = Ring([ps(nc, es, tag + "ps%d" % i, [128, 512]) for i in range(8)])
        s_w = P.slot_sem()
        s_y = P.slot_sem()
        wov = wo_d.rearrange("(c p) n -> p c n", p=128)
        t_w = None
        for c4 in range(4):
            t_w = P.dma("pool", wo[:, c4 * 4:(c4 + 1) * 4, :], wov[:, c4 * 4:(c4 + 1) * 4, :], s_w)
        yv = yT_all_t.ap()

        rk = {}

        def ld(e, i):
            if "r" not in rk:
                rk["r"] = e.partition_id() % 2
            rank = rk["r"]
            ii, r = i // 2, i % 2
            off = ii * 1024 + r * 512
            start = rank * 2048 + off if off else rank * 2048
            return e.dma_start(out=yT[:, i * 4:(i + 1) * 4, :],
                               in_=yv[bass.ds(start, 512), :].rearrange("(cc p) t -> p cc t", p=128))
        t_y = None
        for i in range(4):
            P.cnt[s_y] += 16
            P.q["pool"].append(("i", lambda e, i=i: ld(e, i), s_y, 16))
            t_y = (s_y, P.cnt[s_y])
        for tt in range(NTOK // 128):
            for n in range(D // 512):
                kb, bk, fb = banks.next()
                for c in range(NCH):
                    t_mm = P.op("pe", lambda e, c=c, bk=bk, tt=tt, n=n: e.matmul(
                        bk[:, :], lhsT=yT[:, c, tt * 128:(tt + 1) * 128], rhs=wo[:, c, n * 512:(n + 1) * 512],
                        start=(c == 0), stop=(c == NCH - 1)),
                        waits=[fb, t_w, t_y] if c == 0 else (), sig=(c == NCH - 1))
                kr, xr, fr = xres.next()
                rows = slice(tt * 128, (tt + 1) * 128)
                t_r = P.dma("sp", xr[:, :], x1[rows, n * 512:(n + 1) * 512], s_xr[kr], waits=[fr])
                ko, o, fo = xo.next()
                t_e = P.op("dve", lambda e, o=o, bk=bk, xr=xr: e.tensor_tensor(
                    out=o[:, :], in0=bk[:, :], in1=xr[:, :], op=ALU.add), waits=[t_mm, t_r, fo])
                banks.release(kb, t_e)
                xres.release(kr, t_e)
                t_st = P.dma("sp", x2[rows, n * 512:(n + 1) * 512], o[:, :], s_xo[ko], waits=[t_e])
                xo.release(ko, t_st)
        P.raw("sp", lambda e: None, waits=[xo.free[i] for i in range(3)])
        P.emit()


def final_phase(nc, tag, x_src, gain_t, out):
    with contextlib.ExitStack() as es:
        P = Prog(nc, es, tag)
        gain_bc = sb(nc, es, tag + "gain", [128, D], F32)
        ss = sb(nc, es, tag + "ss", [128, 64], F32)
        rstd = sb(nc, es, tag + "rstd", [128, 64], F32)
        junk = sb(nc, es, tag + "junk", [128, D], BF16)
        xin = Ring([sb(nc, es, tag + "xin%d" % i, [128, D], F32) for i in range(3)])
        s_xin = [P.slot_sem() for _ in range(3)]
        xo = Ring([sb(nc, es, tag + "xo%d" % i, [128, D], F32) for i in range(3)])
        s_xo = [P.slot_sem() for _ in range(3)]
        s_c = P.slot_sem()
        t_g = P.dma("sp", gain_bc[:, :], bcast_rows(gain_t, 0, D), s_c)
        prev_sq = None
        for tt in range(NTOK // 128):
            k, xt, fr = xin.next()
            t = P.dma("sp", xt[:, :], x_src[tt * 128:(tt + 1) * 128, :], s_xin[k], waits=[fr])
            col = tt
            t_sq = P.op("act", lambda e, xt=xt, col=col: e.activation(out=junk[:, :], in_=xt[:, :], func=AF.Square,
                                                                      accum_out=ss[:, col:col + 1]), waits=[t])
            t_a = P.op("dve", lambda e, col=col: e.tensor_scalar(out=rstd[:, col:col + 1], in0=ss[:, col:col + 1],
                                                                 scalar1=1.0 / D, scalar2=EPS, op0=ALU.mult, op1=ALU.add),
                       waits=[t_sq])
            t_a2 = P.op("act", lambda e, col=col: e.sqrt(out=ss[:, col:col + 1], in_=rstd[:, col:col + 1]), waits=[t_a])
            t_b = P.op("dve", lambda e, col=col: e.reciprocal(out=rstd[:, col:col + 1], in_=ss[:, col:col + 1]), waits=[t_a2])
            ko, o, fo = xo.next()
            t_n = P.op("dve", lambda e, o=o, xt=xt, col=col: e.scalar_tensor_tensor(
                out=o[:, :], in0=xt[:, :], scalar=rstd[:, col:col + 1], in1=gain_bc[:, :], op0=ALU.mult, op1=ALU.mult),
                waits=[t_b, fo, t_g])
            xin.release(k, t_n)
            t_st = P.dma("sp", out[tt * 128:(tt + 1) * 128, :], o[:, :], s_xo[ko], waits=[t_n])
            xo.release(ko, t_st)
        P.raw("sp", lambda e: None, waits=[xo.free[i] for i in range(3)])
        P.emit()


def scrub_phase(nc, tag):
    with contextlib.ExitStack() as es:
        P = Prog(nc, es, tag)
        big = [sb(nc, es, tag + "big%d" % i, [128, 12800], F32) for i in range(4)]
        banks = [ps(nc, es, tag + "ps%d" % i, [128, 512]) for i in range(8)]
        for i, b_ in enumerate(big):
            P.op("pool" if i % 2 else "dve", lambda e, b_=b_: e.memset(b_[:, :], 0.0))
        for b_ in banks:
            P.op("dve", lambda e, b_=b_: e.memset(b_[:, :], 0.0))
        P.emit()


def stub_mixer_phase(nc, tag, yT_in_t, yT_all_t, zero_lo=0, zero_hi=1024, dbg=None):
    with contextlib.ExitStack() as es:
        P = Prog(nc, es, tag)
        z = sb(nc, es, tag + "z", [128, NTOK], BF16)
        t_z = P.op("pool", lambda e: e.memset(z[:, :], 0.0))
        s = P.slot_sem()
        t = None
        yv = yT_in_t.ap()
        for hf in range(2):
            for c in range(zero_lo // 128, zero_hi // 128):
                t = P.dma("sp", yv[hf * 1024 + c * 128: hf * 1024 + (c + 1) * 128, :], z[:, :], s, waits=[t_z])
        t_ag = coll_allgather(P, yT_in_t, yT_all_t, [t, t_z])
        P.raw("pool", lambda e: None, waits=[t_ag])
        if dbg is not None:
            sd_ = P.slot_sem()
            td_ = None
            for (dst_, src_) in dbg:
                td_ = P.dma("sp", dst_, src_, sd_, waits=[t_ag])
            P.raw("sp", lambda e: None, waits=[td_])
        P.emit()


STAGE = 3


def build_program(stage=STAGE):
    nc = bass.Bass("TRN2", target_bir_lowering=False)

    def inp(name, shape, dt=F32):
        return nc.dram_tensor(name, shape, dt, kind="ExternalInput")
    x = inp("x", [NTOK, D]).ap()
    n1 = inp("ffn1_norm", [1, D])
    a_w1 = inp("ffn1_w1", [D, DFF]).ap()
    a_w3 = inp("ffn1_w3", [D, DFF]).ap()
    a_w2 = inp("ffn1_w2", [DFF, D]).ap()
    nm = inp("mix_norm", [1, D])
    n2 = inp("ffn2_norm", [1, D])
    c_w1 = inp("ffn2_w1", [D, DFF]).ap()
    c_w3 = inp("ffn2_w3", [D, DFF]).ap()
    c_w2 = inp("ffn2_w2", [DFF, D]).ap()
    nf = inp("final_norm", [1, D])
    wo = inp("w_out_p", [D, D]).ap()
    C = {"ident": inp("ident", [128, 128], BF16).ap()}
    mix_in = declare_mixer_inputs(nc, inp) if stage >= 2 else None
    if stage == 4:
        stage = 3
        s4 = True
    else:
        s4 = False
    out = nc.dram_tensor("out", [NTOK, D], F32, kind="ExternalOutput").ap()

    x1 = nc.dram_tensor("x1", [NTOK, D], F32).ap()
    x2 = nc.dram_tensor("x2", [NTOK, D], F32).ap()
    x3 = nc.dram_tensor("x3", [NTOK, D], F32).ap()
    hT_in = nc.dram_tensor("hT_in", [D, NTOK], BF16)
    hT_all = nc.dram_tensor("hT_all", [2 * D, NTOK], BF16)
    yT_in = nc.dram_tensor("yT_in", [2048, NTOK], BF16)
    yT_all = nc.dram_tensor("yT_all", [4096, NTOK], BF16)

    ffn_phase(nc, "A", C, x, n1, a_w1, a_w3, a_w2, x1)
    normT_phase(nc, "B", C, x1, nm, hT_in, hT_all)
    if stage == 1:
        stub_mixer_phase(nc, "S", yT_in, yT_all)
    else:
        rdbg = None
        if stage == 2:
            rdbg = {}
            for b_ in (0, 1):
                rdbg["qk%d" % b_] = nc.dram_tensor("dbg_qk%d" % b_, [128, 2, 2, 2, 512], BF16, kind="ExternalOutput").ap()
                rdbg["qx%d" % b_] = nc.dram_tensor("dbg_qx%d" % b_, [128, 2, 2, 512], BF16, kind="ExternalOutput").ap()
                rdbg["R%d" % b_] = nc.dram_tensor("dbg_R%d" % b_, [128, 2, 512], F32, kind="ExternalOutput").ap()
                rdbg["Rb%d" % b_] = nc.dram_tensor("dbg_Rb%d" % b_, [128, 2, 512], BF16, kind="ExternalOutput").ap()
                rdbg["v%d" % b_] = nc.dram_tensor("dbg_v%d" % b_, [128, 4, 512], BF16, kind="ExternalOutput").ap()
        ret_phase(nc, "R", C, mix_in, hT_all, yT_in, dbg=rdbg)
        if stage == 2:
            d1 = nc.dram_tensor("dbg_yT", [2048, NTOK], BF16, kind="ExternalOutput").ap()
            d2 = nc.dram_tensor("dbg_hT", [4096, NTOK], BF16, kind="ExternalOutput").ap()
            stub_mixer_phase(nc, "S", yT_in, yT_all, 512, 1024, dbg=[(d1, yT_in.ap()), (d2, hT_all.ap())])
        else:
            scrub_phase(nc, "Z")
            nsa_phase(nc, "N", C, mix_in, hT_all, yT_in, yT_all, do_ag=False)
            stub_mixer_phase(nc, "S", yT_in, yT_all, 0, 0)
    wout_phase(nc, "W", C, yT_all, wo, x1, x2)
    if s4:
        final_phase(nc, "F", x2, nf, out)
        return nc
    ffn_phase(nc, "C", C, x2, n2, c_w1, c_w3, c_w2, x3)
    final_phase(nc, "F", x3, nf, out)
    return nc


_NC_CACHE = {}


def host_consts():
    import ml_dtypes
    c = {"ident": np.eye(128, dtype=np.float32).astype(ml_dtypes.bfloat16)}
    return c


def build_launch(which):
    nc = bass.Bass("TRN2", target_bir_lowering=False)

    def inp(name, shape, dt=F32):
        return nc.dram_tensor(name, shape, dt, kind="ExternalInput")
    C = {"ident": inp("ident", [128, 128], BF16).ap()}
    if which == "L1":
        x = inp("x", [NTOK, D]).ap()
        n1 = inp("ffn1_norm", [1, D])
        w1 = inp("ffn1_w1", [D, DFF]).ap()
        w3 = inp("ffn1_w3", [D, DFF]).ap()
        w2 = inp("ffn1_w2", [DFF, D]).ap()
        out = nc.dram_tensor("out", [NTOK, D], F32, kind="ExternalOutput").ap()
        ffn_phase(nc, "A", C, x, n1, w1, w3, w2, out)
    elif which == "L23":
        x1 = inp("x", [NTOK, D]).ap()
        nm = inp("mix_norm", [1, D])
        wo = inp("w_out_p", [D, D]).ap()
        mix_in = declare_mixer_inputs(nc, inp)
        n2 = inp("ffn2_norm", [1, D])
        w1 = inp("ffn2_w1", [D, DFF]).ap()
        w3 = inp("ffn2_w3", [D, DFF]).ap()
        w2 = inp("ffn2_w2", [DFF, D]).ap()
        nf = inp("final_norm", [1, D])
        out = nc.dram_tensor("out", [NTOK, D], F32, kind="ExternalOutput").ap()
        hT_in = nc.dram_tensor("hT_in", [D, NTOK], BF16)
        hT_all = nc.dram_tensor("hT_all", [2 * D, NTOK], BF16)
        yT_in = nc.dram_tensor("yT_in", [2048, NTOK], BF16)
        yT_all = nc.dram_tensor("yT_all", [4096, NTOK], BF16)
        x2 = nc.dram_tensor("x2", [NTOK, D], F32).ap()
        x3 = nc.dram_tensor("x3", [NTOK, D], F32).ap()
        normT_phase(nc, "B", C, x1, nm, hT_in, hT_all)
        ret_phase(nc, "R", C, mix_in, hT_all, yT_in)
        nsa_phase(nc, "N", C, mix_in, hT_all, yT_in, yT_all)
        wout_phase(nc, "W", C, yT_all, wo, x1, x2)
        ffn_phase(nc, "C", C, x2, n2, w1, w3, w2, x3)
        final_phase(nc, "F", x3, nf, out)
    elif which == "L2":
        x1 = inp("x", [NTOK, D]).ap()
        nm = inp("mix_norm", [1, D])
        wo = inp("w_out_p", [D, D]).ap()
        mix_in = declare_mixer_inputs(nc, inp)
        out = nc.dram_tensor("out", [NTOK, D], F32, kind="ExternalOutput").ap()
        hT_in = nc.dram_tensor("hT_in", [D, NTOK], BF16)
        hT_all = nc.dram_tensor("hT_all", [2 * D, NTOK], BF16)
        yT_in = nc.dram_tensor("yT_in", [2048, NTOK], BF16)
        yT_all = nc.dram_tensor("yT_all", [4096, NTOK], BF16)
        normT_phase(nc, "B", C, x1, nm, hT_in, hT_all)
        ret_phase(nc, "R", C, mix_in, hT_all, yT_in)
        nsa_phase(nc, "N", C, mix_in, hT_all, yT_in, yT_all)
        wout_phase(nc, "W", C, yT_all, wo, x1, out)
    else:
        x2 = inp("x", [NTOK, D]).ap()
        n2 = inp("ffn2_norm", [1, D])
        w1 = inp("ffn2_w1", [D, DFF]).ap()
        w3 = inp("ffn2_w3", [D, DFF]).ap()
        w2 = inp("ffn2_w2", [DFF, D]).ap()
        nf = inp("final_norm", [1, D])
        out = nc.dram_tensor("out", [NTOK, D], F32, kind="ExternalOutput").ap()
        x3 = nc.dram_tensor("x3", [NTOK, D], F32).ap()
        ffn_phase(nc, "C", C, x2, n2, w1, w3, w2, x3)
        final_phase(nc, "F", x3, nf, out)
    return nc


SPLIT = False
LAUNCHES = ("L1", "L23")


def kernel(x, ffn1_norm, ffn1_w1, ffn1_w3, ffn1_w2, mix_norm, w_in, ret_gn_gain,
           cmp_pe_k, cmp_w1_k, cmp_w2_k, cmp_pe_v, cmp_w1_v, cmp_w2_v, w_out,
           ffn2_norm, ffn2_w1, ffn2_w3, ffn2_w2, rel_bias, final_norm, _stage=None):
    stage = STAGE if _stage is None else _stage
    f = lambda a: np.ascontiguousarray(np.asarray(a, dtype=np.float32))
    x = f(x)
    w_in0 = f(w_in)[0]
    w_out0 = f(w_out)[0]
    consts = host_consts()
    if SPLIT and _stage is None:
        xs = [np.ascontiguousarray(x[c // 2, (c % 2) * NTOK:(c % 2 + 1) * NTOK]) for c in range(8)]
        for which in LAUNCHES:
            if which not in _NC_CACHE:
                _NC_CACHE[which] = build_launch(which)
            nc = _NC_CACHE[which]
            if which == "L1":
                common = {"ffn1_norm": f(ffn1_norm).reshape(1, D), "ffn1_w1": f(ffn1_w1)[0], "ffn1_w3": f(ffn1_w3)[0],
                          "ffn1_w2": f(ffn1_w2)[0]}
            elif which == "L2":
                common = {"mix_norm": f(mix_norm).reshape(1, D), "w_out_p": w_out0}
            elif which == "L23":
                common = {"mix_norm": f(mix_norm).reshape(1, D), "w_out_p": w_out0,
                          "ffn2_norm": f(ffn2_norm).reshape(1, D), "ffn2_w1": f(ffn2_w1)[0], "ffn2_w3": f(ffn2_w3)[0],
                          "ffn2_w2": f(ffn2_w2)[0], "final_norm": f(final_norm).reshape(1, D)}
            else:
                common = {"ffn2_norm": f(ffn2_norm).reshape(1, D), "ffn2_w1": f(ffn2_w1)[0], "ffn2_w3": f(ffn2_w3)[0],
                          "ffn2_w2": f(ffn2_w2)[0], "final_norm": f(final_norm).reshape(1, D)}
            common.update(consts)
            in_maps = []
            for c in range(8):
                m = dict(common)
                m["x"] = xs[c]
                if which in ("L2", "L23"):
                    m.update(mixer_host_inputs(c % 2, w_in0, f(ret_gn_gain)[0], f(cmp_pe_k)[0], f(cmp_w1_k)[0], f(cmp_w2_k)[0],
                                               f(cmp_pe_v)[0], f(cmp_w1_v)[0], f(cmp_w2_v)[0], f(rel_bias)))
                in_maps.append(m)
            res = run_bass_kernel_spmd(nc, in_maps, core_ids=list(range(8)))
            xs = [np.ascontiguousarray(np.asarray(res.results[c]["out"], dtype=np.float32)) for c in range(8)]
        outp = np.empty((4, SEQ, D), np.float32)
        for c in range(8):
            outp[c // 2, (c % 2) * NTOK:(c % 2 + 1) * NTOK] = xs[c]
        return outp
    if stage not in _NC_CACHE:
        _NC_CACHE[stage] = build_program(stage)
    nc = _NC_CACHE[stage]
    perm = np.arange(2048)
    common = {
        "ffn1_norm": f(ffn1_norm).reshape(1, D), "ffn1_w1": f(ffn1_w1)[0], "ffn1_w3": f(ffn1_w3)[0], "ffn1_w2": f(ffn1_w2)[0],
        "mix_norm": f(mix_norm).reshape(1, D),
        "ffn2_norm": f(ffn2_norm).reshape(1, D), "ffn2_w1": f(ffn2_w1)[0], "ffn2_w3": f(ffn2_w3)[0], "ffn2_w2": f(ffn2_w2)[0],
        "final_norm": f(final_norm).reshape(1, D),
        "w_out_p": np.ascontiguousarray(w_out0[perm]),
    }
    common.update(consts)
    in_maps = []
    for c in range(8):
        b, j = c // 2, c % 2
        m = dict(common)
        m["x"] = np.ascontiguousarray(x[b, j * NTOK:(j + 1) * NTOK])
        if stage >= 2:
            m.update(mixer_host_inputs(j, w_in0, f(ret_gn_gain)[0], f(cmp_pe_k)[0], f(cmp_w1_k)[0], f(cmp_w2_k)[0],
                                       f(cmp_pe_v)[0], f(cmp_w1_v)[0], f(cmp_w2_v)[0], f(rel_bias)))
        in_maps.append(m)
    res = run_bass_kernel_spmd(nc, in_maps, core_ids=list(range(8)))
    global _LAST_RES
    _LAST_RES = res
    outp = np.empty((4, SEQ, D), np.float32)
    for c in range(8):
        b, j = c // 2, c % 2
        outp[b, j * NTOK:(j + 1) * NTOK] = res.results[c]["out"]
    return outp


def declare_mixer_inputs(nc, inp):
    m = {}
    m["w_ret"] = inp("w_ret", [D, 2048]).ap()
    m["gn_gain"] = inp("gn_gain", [1, 512])
    m["cosT"] = inp("cosT", [128, SEQ]).ap()
    m["sinT"] = inp("sinT", [128, SEQ]).ap()
    m["decT"] = inp("decT", [128, 2, 128]).ap()
    m["xi4"] = inp("xi4", [128, 2, 512]).ap()
    m["zeta"] = inp("zeta", [128, 2]).ap()
    m["cdec"] = inp("cdec", [128, 2]).ap()
    nsa_declare(nc, inp, m)
    return m


def ret_host_inputs(g, w_in0, gn_gain):
    hs = [2 * g, 2 * g + 1]
    cols = []
    for base in (0, 1024, 2048, 3072):
        for h in hs:
            cols.append(np.arange(base + h * 256, base + (h + 1) * 256))
    cols = np.concatenate(cols)
    m = {"w_ret": np.ascontiguousarray(w_in0[:, cols]),
         "gn_gain": np.ascontiguousarray(gn_gain[hs[0] * 256:(hs[1] + 1) * 256].reshape(1, 512))}
    half = 128
    inv = (10000.0 ** (-np.arange(half, dtype=np.float32) / np.float32(half))).astype(np.float32)
    ang = np.arange(SEQ, dtype=np.float32)[:, None] * inv[None, :]
    m["cosT"] = np.ascontiguousarray(np.cos(ang).T.astype(np.float32))
    m["sinT"] = np.ascontiguousarray(np.sin(ang).T.astype(np.float32))
    ks = 256.0 ** -0.5
    decT = np.zeros((128, 2, 128), np.float32)
    xi4 = np.zeros((128, 2, 512), np.float32)
    zeta = np.zeros((128, 2), np.float32)
    cdec = np.zeros((128, 2), np.float32)
    idx = np.arange(128, dtype=np.float64)
    for i, h in enumerate(hs):
        lg = np.log(1.0 - 2.0 ** (-5.0 - h))
        diff = idx[None, :] - idx[:, None]
        decT[:, i, :] = np.where(diff >= 0, np.exp(np.maximum(diff, 0) * lg), 0.0) * ks
        xi4[:, i, :] = np.tile(np.exp((idx + 1.0) * lg), 4)[None, :]
        zeta[:, i] = np.exp((127.0 - idx) * lg) * ks
        cdec[:, i] = np.exp(128.0 * lg)
    m.update({"decT": decT, "xi4": xi4, "zeta": zeta, "cdec": cdec})
    return m


def load_hT_block(P, eng, buf, hv, r, col0, bt, sem, waits):
    t = None
    for k in range(4):
        t = P.dma(eng, buf[:, k * 4:(k + 1) * 4, :],
                  hv[k * 1024 + r * 512: k * 1024 + (r + 1) * 512, col0:col0 + bt].rearrange("(c p) t -> p c t", p=128),
                  sem, waits=waits if k == 0 else ())
    return t


def ret_phase(nc, tag, C, M, hT_all_t, yT_in_t, dbg=None):
    BT = 512
    with contextlib.ExitStack() as es:
        P = Prog(nc, es, tag)
        w = sb(nc, es, tag + "w", [128, NCH, 2048], BF16)
        hc = Ring([sb(nc, es, tag + "hc%d" % i, [128, NCH, BT], BF16) for i in range(2)])
        s_hc = [P.slot_sem() for _ in range(2)]
        cs = Ring([sb(nc, es, tag + "cs%d" % i, [128, 2, BT], F32) for i in range(2)])
        s_cs = [P.slot_sem() for _ in range(2)]
        ident = sb(nc, es, tag + "ident", [128, 128], BF16)
        gain_bc = sb(nc, es, tag + "gain", [128, 512], F32)
        decT = sb(nc, es, tag + "decT", [128, 2, 128], F32)
        xi4 = sb(nc, es, tag + "xi4", [128, 2, 512], F32)
        zeta = sb(nc, es, tag + "zeta", [128, 2], F32)
        cdec = sb(nc, es, tag + "cdec", [128, 2], F32)
        qk = sb(nc, es, tag + "qk", [128, 2, 2, 2, BT], BF16)
        qx = sb(nc, es, tag + "qx", [128, 2, 2, BT], BF16)
        vsb = sb(nc, es, tag + "v", [128, 4, 512], BF16)
        gs = sb(nc, es, tag + "gs", [128, 512], F32)
        gsg = sb(nc, es, tag + "gsg", [128, 4, 512], F32)
        tmp = [sb(nc, es, tag + "tmp%d" % i, [128, BT], F32) for i in range(4)]
        R32 = sb(nc, es, tag + "R32", [128, 2, 512], F32)
        Rb = sb(nc, es, tag + "Rb", [128, 2, 512], BF16)
        sd = Ring([sb(nc, es, tag + "sd%d" % i, [128, 128], BF16) for i in range(2)])
        kz = Ring([sb(nc, es, tag + "kz%d" % i, [128, 256], BF16) for i in range(2)])
        st6 = sb(nc, es, tag + "st6", [128, 8, 6], F32)
        mv = sb(nc, es, tag + "mv", [128, 8, 4], F32)
        yn = Ring([sb(nc, es, tag + "yn%d" % i, [128, 256], F32) for i in range(2)])
        yo = Ring([sb(nc, es, tag + "yo%d" % i, [128, 256], BF16) for i in range(2)])
        ystage = Ring([sb(nc, es, tag + "ys%d" % i, [128, 4, BT], BF16) for i in range(2)])
        s_ys = [P.slot_sem() for _ in range(2)]
        banks = Ring([ps(nc, es, tag + "ps%d" % i, [128, 512]) for i in range(8)])

        s_c = P.slot_sem()
        wv = M["w_ret"].rearrange("(c p) n -> p c n", p=128)
        s_w = P.slot_sem()
        t_w = None
        for c4 in range(4):
            t_w = P.dma("pool", w[:, c4 * 4:(c4 + 1) * 4, :], wv[:, c4 * 4:(c4 + 1) * 4, :], s_w)
        P.dma("sp", ident[:, :], C["ident"], s_c)
        P.dma("sp", gain_bc[:, :], bcast_rows(M["gn_gain"], 0, 512), s_c)
        P.dma("sp", decT[:, :, :], M["decT"], s_c)
        P.dma("sp", xi4[:, :, :], M["xi4"], s_c)
        P.dma("sp", zeta[:, :], M["zeta"], s_c)
        t_c = P.dma("sp", cdec[:, :], M["cdec"], s_c)
        t_r0 = P.op("pool", lambda e: e.memset(R32[:, :, :], 0.0))
        t_rb0 = P.op("pool", lambda e: e.memset(Rb[:, :, :], 0.0))
        hv = hT_all_t.ap()
        yv = yT_in_t.ap()
        NBLK = SEQ // BT

        def load_blk(b):
            T0 = b * BT
            r, col0 = T0 // NTOK, T0 % NTOK
            k, buf, fr = hc.next()
            t1 = load_hT_block(P, "sp", buf, hv, r, col0, BT, s_hc[k], [fr])
            k2, cb, fr2 = cs.next()
            P.dma("sp", cb[:, 0, :], M["cosT"][:, T0:T0 + BT], s_cs[k2], waits=[fr2])
            t2 = P.dma("sp", cb[:, 1, :], M["sinT"][:, T0:T0 + BT], s_cs[k2])
            return k, buf, t1, k2, cb, t2

        Rb_ready = [t_rb0, t_rb0]
        R32_ready = [t_r0, t_r0]
        qk_free = [None]
        tmp_free = [None] * 4
        nxt = load_blk(0)
        for b in range(NBLK):
            k, hcb, t_h, k2, cb, t_cs = nxt
            if b + 1 < NBLK:
                nxt = load_blk(b + 1)
            T0 = b * BT
            last_proj = None
            rot_done = []
            for which in range(2):
                for h in range(2):
                    bk = []
                    for dc in range(2):
                        col0 = which * 512 + h * 256 + dc * 128
                        kb, bank, fb = banks.next()
                        for c in range(NCH):
                            tm = P.op("pe", lambda e, c=c, bank=bank, col0=col0, hcb=hcb: e.matmul(
                                bank[:, :], lhsT=w[:, c, col0:col0 + 128], rhs=hcb[:, c, :],
                                start=(c == 0), stop=(c == NCH - 1)),
                                waits=[t_w, t_h, fb] if c == 0 else (), sig=(c == NCH - 1))
                        bk.append((kb, bank, tm))
                    last_proj = bk[1][2]
                    (k1_, x1, tx1), (k2_, x2, tx2) = bk
                    o1 = qk[:, which, h, 0, :]
                    o2 = qk[:, which, h, 1, :]
                    ta = P.op("dve", lambda e, x1=x1: e.tensor_tensor(out=tmp[0][:, :], in0=x1[:, :], in1=cb[:, 0, :], op=ALU.mult),
                              waits=[tx1, t_cs, tmp_free[0]])
                    tb = P.op("dve", lambda e, x2=x2: e.tensor_tensor(out=tmp[1][:, :], in0=x2[:, :], in1=cb[:, 1, :], op=ALU.mult),
                              waits=[tx2, tmp_free[1]])
                    tc_ = P.op("dve", lambda e, x1=x1: e.tensor_tensor(out=tmp[2][:, :], in0=x1[:, :], in1=cb[:, 1, :], op=ALU.mult),
                               waits=[tmp_free[2]])
                    td = P.op("dve", lambda e, x2=x2: e.tensor_tensor(out=tmp[3][:, :], in0=x2[:, :], in1=cb[:, 0, :], op=ALU.mult),
                              waits=[tmp_free[3]])
                    banks.release(k1_, td)
                    banks.release(k2_, td)
                    te = P.op("pool", lambda e, o1=o1: e.tensor_tensor(out=o1, in0=tmp[0][:, :], in1=tmp[1][:, :], op=ALU.subtract),
                              waits=[ta, tb, qk_free[0]])
                    tf = P.op("pool", lambda e, o2=o2: e.tensor_tensor(out=o2, in0=tmp[2][:, :], in1=tmp[3][:, :], op=ALU.add),
                              waits=[tc_, td])
                    tmp_free[0] = tmp_free[1] = te
                    tmp_free[2] = tmp_free[3] = tf
                    rot_done += [te, tf]
                    if which == 0:
                        for dc, tsrc in ((0, te), (1, tf)):
                            tq = P.op("pool", lambda e, dc=dc, h=h: e.tensor_tensor(
                                out=qx[:, h, dc, :], in0=qk[:, 0, h, dc, :], in1=xi4[:, h, :], op=ALU.mult),
                                waits=[tsrc, t_c])
                            rot_done.append(tq)
            vg_done = []
            for t in range(4):
                kb, bank, fb = banks.next()
                for c in range(NCH):
                    tm = P.op("pe", lambda e, c=c, bank=bank, t=t, hcb=hcb: e.matmul(
                        bank[:, :], lhsT=hcb[:, c, t * 128:(t + 1) * 128], rhs=w[:, c, 1024:1536],
                        start=(c == 0), stop=(c == NCH - 1)), waits=[fb] if c == 0 else (), sig=(c == NCH - 1))
                tv = P.op("act", lambda e, bank=bank, t=t: e.copy(out=vsb[:, t, :], in_=bank[:, :]), waits=[tm, qk_free[0]])
                banks.release(kb, tv)
                kb, bank, fb = banks.next()
                for c in range(NCH):
                    tm = P.op("pe", lambda e, c=c, bank=bank, t=t, hcb=hcb: e.matmul(
                        bank[:, :], lhsT=hcb[:, c, t * 128:(t + 1) * 128], rhs=w[:, c, 1536:2048],
                        start=(c == 0), stop=(c == NCH - 1)), waits=[fb] if c == 0 else (), sig=(c == NCH - 1))
                last_proj = tm
                tg = P.op("act", lambda e, bank=bank: e.activation(out=gs[:, :], in_=bank[:, :], func=AF.Silu),
                          waits=[tm, vg_done[-1] if vg_done else None])
                banks.release(kb, tg)
                tg2 = P.op("dve", lambda e, t=t: e.tensor_tensor(out=gsg[:, t, :], in0=gs[:, :], in1=gain_bc[:, :], op=ALU.mult),
                           waits=[tg, t_c, qk_free[0]])
                vg_done += [tv, tg2]
            hc.release(k, last_proj)
            cs.release(k2, rot_done[-1])
            if dbg is not None and b in (0, 1):
                sdb_ = P.slot_sem()
                P.dma("sp", dbg["qk%d" % b], qk[:, :, :, :, :], sdb_, waits=[rot_done, vg_done])
                P.dma("sp", dbg["qx%d" % b], qx[:, :, :, :], sdb_)
                P.dma("sp", dbg["R%d" % b], R32[:, :, :], sdb_, waits=[R32_ready])
                P.dma("sp", dbg["Rb%d" % b], Rb[:, :, :], sdb_, waits=[Rb_ready])
                tdb_ = P.dma("sp", dbg["v%d" % b], vsb[:, :, :], sdb_)
                P.raw("sp", lambda e: None, waits=[tdb_])
                P.raw("pe", lambda e: None, waits=[tdb_])
                P.raw("dve", lambda e: None, waits=[tdb_])
                P.raw("pool", lambda e: None, waits=[tdb_])
                P.raw("act", lambda e: None, waits=[tdb_])
            kys, ysb, fys = ystage.next()
            ys_done = []
            last_pe = None
            for t in range(4):
                tsl = slice(t * 128, (t + 1) * 128)
                for h in range(2):
                    kbs, bs, fbs = banks.next()
                    for dc in range(2):
                        t_s = P.op("pe", lambda e, dc=dc, bs=bs, h=h, tsl=tsl: e.matmul(
                            bs[:, 0:128], lhsT=qk[:, 1, h, dc, tsl], rhs=qk[:, 0, h, dc, tsl],
                            start=(dc == 0), stop=(dc == 1)), waits=[fbs, rot_done] if dc == 0 else (), sig=(dc == 1))
                    ksd, sdb, fsd = sd.next()
                    t_sd = P.op("dve", lambda e, bs=bs, sdb=sdb, h=h: e.tensor_tensor(
                        out=sdb[:, :], in0=bs[:, 0:128], in1=decT[:, h, :], op=ALU.mult), waits=[t_s, fsd, t_c])
                    banks.release(kbs, t_sd)
                    kbk, bkz, fbk = banks.next()
                    bkzb = bkz[:, :].bitcast(BF16)
                    for dc in range(2):
                        t_kt = P.op("pe", lambda e, dc=dc, bkzb=bkzb, h=h, tsl=tsl: e.transpose(
                            out=bkzb[:, dc * 128:(dc + 1) * 128], in_=qk[:, 1, h, dc, tsl], identity=ident[:, :]),
                            waits=[fbk] if dc == 0 else (), sig=(dc == 1))
                    kkz, kzb, fkz = kz.next()
                    t_kz = P.op("act", lambda e, bkzb=bkzb, kzb=kzb, h=h: e.mul(out=kzb[:, :], in_=bkzb[:, 0:256], mul=zeta[:, h:h + 1]),
                                waits=[t_kt, fkz, t_c])
                    banks.release(kbk, t_kz)
                    kby, by, fby = banks.next()
                    P.op("pe", lambda e, by=by, sdb=sdb, t=t, h=h: e.matmul(
                        by[:, 0:256], lhsT=sdb[:, :], rhs=vsb[:, t, h * 256:(h + 1) * 256], start=True, stop=False),
                        waits=[fby, t_sd, vg_done], sig=False)
                    for dc in range(2):
                        t_y = P.op("pe", lambda e, dc=dc, by=by, h=h, tsl=tsl: e.matmul(
                            by[:, 0:256], lhsT=qx[:, h, dc, tsl], rhs=Rb[:, h, dc * 256:(dc + 1) * 256],
                            start=False, stop=(dc == 1)), waits=[Rb_ready[h]] if dc == 0 else (), sig=(dc == 1))
                    sd.release(ksd, t_y)
                    kbu, bu, fbu = banks.next()
                    for dc in range(2):
                        t_u = P.op("pe", lambda e, dc=dc, bu=bu, kzb=kzb, t=t, h=h: e.matmul(
                            bu[:, dc * 256:(dc + 1) * 256], lhsT=kzb[:, dc * 128:(dc + 1) * 128],
                            rhs=vsb[:, t, h * 256:(h + 1) * 256], start=True, stop=True),
                            waits=[fbu, t_kz] if dc == 0 else (), sig=(dc == 1))
                    kz.release(kkz, t_u)
                    t_R = P.op("dve", lambda e, bu=bu, h=h: e.scalar_tensor_tensor(
                        out=R32[:, h, :], in0=R32[:, h, :], scalar=cdec[:, h:h + 1], in1=bu[:, :],
                        op0=ALU.mult, op1=ALU.add), waits=[t_u, R32_ready[h]])
                    banks.release(kbu, t_R)
                    R32_ready[h] = t_R
                    t_Rb = P.op("pool", lambda e, h=h: e.tensor_copy(out=Rb[:, h, :], in_=R32[:, h, :]), waits=[t_R, t_y])
                    Rb_ready[h] = t_Rb
                    R32_ready[h] = [t_R, t_Rb]
                    col = (t * 2 + h)
                    t_bs = P.op("dve", lambda e, by=by, col=col: e.bn_stats(out=st6[:, col, :], in_=by[:, 0:256]), waits=[t_y])
                    t_ba = P.op("dve", lambda e, col=col: e.bn_aggr(out=mv[:, col, 0:2], in_=st6[:, col, :]), waits=[t_bs])
                    t_v1 = P.op("dve", lambda e, col=col: e.tensor_scalar(out=mv[:, col, 2:3], in0=mv[:, col, 1:2],
                                                                          scalar1=EPS, scalar2=None, op0=ALU.add), waits=[t_ba])
                    t_v2 = P.op("act", lambda e, col=col: e.sqrt(out=mv[:, col, 3:4], in_=mv[:, col, 2:3]), waits=[t_v1])
                    t_v3 = P.op("dve", lambda e, col=col: e.reciprocal(out=mv[:, col, 2:3], in_=mv[:, col, 3:4]), waits=[t_v2])
                    kyn, ynb, fyn = yn.next()
                    t_yn = P.op("dve", lambda e, by=by, ynb=ynb, col=col: e.tensor_scalar(
                        out=ynb[:, :], in0=by[:, 0:256], scalar1=mv[:, col, 0:1], scalar2=mv[:, col, 2:3],
                        op0=ALU.subtract, op1=ALU.mult), waits=[t_v3, fyn])
                    banks.release(kby, t_yn)
                    kyo, yob, fyo = yo.next()
                    t_yo = P.op("pool", lambda e, ynb=ynb, yob=yob, t=t, h=h: e.tensor_tensor(
                        out=yob[:, :], in0=ynb[:, :], in1=gsg[:, t, h * 256:(h + 1) * 256], op=ALU.mult),
                        waits=[t_yn, fyo, vg_done])
                    yn.release(kyn, t_yo)
                    kbt, bt, fbt = banks.next()
                    btb = bt[:, :].bitcast(BF16)
                    for ec in range(2):
                        t_t = P.op("pe", lambda e, ec=ec, btb=btb, yob=yob: e.transpose(
                            out=btb[:, ec * 128:(ec + 1) * 128], in_=yob[:, ec * 128:(ec + 1) * 128], identity=ident[:, :]),
                            waits=[fbt, t_yo] if ec == 0 else (), sig=(ec == 1))
                    last_pe = t_t
                    yo.release(kyo, t_t)
                    t_cp = P.op("act", lambda e, btb=btb, ysb=ysb, h=h, tsl=tsl: e.copy(
                        out=ysb[:, h * 2:(h + 1) * 2, tsl], in_=btb[:, 0:256].rearrange("p (e c) -> p e c", e=2)),
                        waits=[t_t, fys])
                    banks.release(kbt, t_cp)
                    ys_done.append(t_cp)
            qk_free[0] = last_pe
            half, col0 = T0 // NTOK, T0 % NTOK
            t_st = None
            for r4 in range(4):
                t_st = P.dma("sp", yv[half * 1024 + r4 * 128: half * 1024 + (r4 + 1) * 128, col0:col0 + BT],
                             ysb[:, r4, :], s_ys[kys], waits=[ys_done] if r4 == 0 else ())
            ystage.release(kys, t_st)
        P.raw("sp", lambda e: None, waits=[ystage.free[0], ystage.free[1]])
        P.emit()


def mixer_host_inputs(g, w_in0, gn_gain, pe_k, w1_k, w2_k, pe_v, w1_v, w2_v, rel_bias):
    m = ret_host_inputs(g, w_in0, gn_gain)
    m.update(nsa_host_inputs(g, w_in0, pe_k, w1_k, w2_k, pe_v, w1_v, w2_v, rel_bias))
    return m


NW = 1292
O_KC, O_VC, O_KS, O_KW, O_VS, O_VW, O_GT = 512, 640, 768, 896, 1024, 1152, 1280
NEGB = -1.0e30


def t5_bucket_np(rel):
    n = np.maximum(rel, 0)
    nf = np.maximum(n, 1).astype(np.float32)
    large = 16 + (np.log(nf / np.float32(16)) / np.float32(math.log(128 / 16)) * np.float32(16)).astype(np.int32)
    large = np.minimum(large, 31)
    return np.where(n < 16, n, large)


def nsa_declare(nc, inp, m):
    m["w_nsa"] = inp("w_nsa", [D, NW]).ap()
    for nm in ("k", "v"):
        m["peT_" + nm] = inp("peT_" + nm, [128, 32]).ap()
        m["w1_" + nm] = inp("w1_" + nm, [128, 32, 128]).ap()
        m["w2_" + nm] = inp("w2_" + nm, [128, 128]).ap()
    m["rb_bc"] = inp("rb_bc", [128, 4, 32]).ap()
    m["bk_c"] = inp("bk_c", [128, 16]).ap()
    m["bk_0"] = inp("bk_0", [128, 128]).ap()
    m["bk_1"] = inp("bk_1", [128, 128]).ap()
    m["wm4"] = inp("wm4", [128, 128], BF16).ap()
    m["E_all"] = inp("E_all", [64, 32, 128], BF16).ap()
    m["selC"] = inp("selC", [32, 128, 64]).ap()
    m["selF"] = inp("selF", [32, 128, 64]).ap()
    m["ovl"] = inp("ovl", [128, 2, 64], BF16).ap()


def nsa_host_inputs(g, w_in0, pe_k, w1_k, w2_k, pe_v, w1_v, w2_v, rel_bias):
    import ml_dtypes
    bf = ml_dtypes.bfloat16
    cols = [np.arange(4096 + g * 512, 4096 + (g + 1) * 512)]
    for base in (5120, 5376, 5632, 6144, 5888, 6400):
        cols.append(np.arange(base + g * 128, base + (g + 1) * 128))
    cols.append(np.arange(6656 + 12 * g, 6656 + 12 * (g + 1)))
    cols = np.concatenate(cols)
    m = {"w_nsa": np.ascontiguousarray(w_in0[:, cols])}
    for nm, pe, w1, w2 in (("k", pe_k, w1_k, w2_k), ("v", pe_v, w1_v, w2_v)):
        m["peT_" + nm] = np.ascontiguousarray(pe.T)
        m["w1_" + nm] = np.ascontiguousarray(np.transpose(w1, (1, 0, 2)))
        m["w2_" + nm] = np.ascontiguousarray(w2)
    m["rb_bc"] = np.ascontiguousarray(np.broadcast_to(rel_bias[4 * g:4 * g + 4][None], (128, 4, 32)))
    i = np.arange(128)
    r = np.arange(-9, 7)
    d = i[:, None] - 16 * r[None, :] - 31
    m["bk_c"] = np.where(d >= 0, t5_bucket_np(d), -1).astype(np.float32)
    t = np.arange(128)[:, None]
    s = np.arange(128)[None, :]
    d0 = s - t
    m["bk_0"] = np.where(d0 >= 0, t5_bucket_np(d0), -1).astype(np.float32)
    m["bk_1"] = t5_bucket_np(128 + s - t).astype(np.float32)
    m["wm4"] = np.where(s >= t, -30000.0, 0.0).astype(np.float32).astype(bf)
    E = np.zeros((64, 32, 128), np.float32)
    for kt in range(32):
        for tl in range(128):
            E[2 * kt + tl // 64, kt, tl] = 1.0
    m["E_all"] = E.astype(bf)
    pos = np.arange(SEQ)
    cur = pos // 64
    blk = np.arange(64)
    causal = (blk[None, :] * 64) <= pos[:, None]
    forced = (blk[None, :] == 0) | (blk[None, :] == cur[:, None]) | (blk[None, :] == cur[:, None] - 1)
    selC = (causal & ~forced).astype(np.float32)
    selF = np.where(forced, 1.0e4 + blk[None, :].astype(np.float32), np.where(causal, 0.0, NEGB)).astype(np.float32)
    m["selC"] = np.ascontiguousarray(selC.reshape(32, 128, 64))
    m["selF"] = np.ascontiguousarray(selF.reshape(32, 128, 64))
    n_cmp = 255
    cmp_idx = np.arange(n_cmp)[:, None] * 16 + np.arange(32)[None, :]
    sel_of = cmp_idx // 64
    overlap = (sel_of[:, :, None] == np.arange(64)[None, None, :]).sum(1).astype(np.float32) / 32.0
    ov = np.zeros((256, 64), np.float32)
    ov[:255] = overlap
    m["ovl"] = np.ascontiguousarray(ov.reshape(2, 128, 64).transpose(1, 0, 2)).astype(bf)
    return m


def nsa_phase(nc, tag, C, M, hT_all_t, yT_in_t, yT_all_t, do_ag=True, nqt=32, branches=(0, 1)):
    BT = 256
    SCALE = 128.0 ** -0.5
    with contextlib.ExitStack() as es:
        P = Prog(nc, es, tag)
        w = sb(nc, es, tag + "w", [128, NCH, NW], BF16)
        hc = Ring([sb(nc, es, tag + "hc%d" % i, [128, NCH, BT], BF16) for i in range(2)])
        s_hc = [P.slot_sem() for _ in range(2)]
        qT = sb(nc, es, tag + "qT", [128, 4, SEQ], BF16)
        kvT = sb(nc, es, tag + "kvT", [128, 4, SEQ], BF16)
        vaug = sb(nc, es, tag + "vaug", [128, 2, 32, 130], BF16)
        sig = sb(nc, es, tag + "sig", [128, 32, 12], F32)
        ident = sb(nc, es, tag + "ident", [128, 128], BF16)
        w1 = [sb(nc, es, tag + "w1%d" % i, [128, 32, 128], BF16) for i in range(2)]
        w2 = [sb(nc, es, tag + "w2%d" % i, [128, 128], BF16) for i in range(2)]
        peT = [sb(nc, es, tag + "peT%d" % i, [128, 32], BF16) for i in range(2)]
        cvec = sb(nc, es, tag + "cvec", [128, 2], F32)
        hdn = [sb(nc, es, tag + "hdn%d" % i, [128, 256], BF16) for i in range(2)]
        kcmpT = sb(nc, es, tag + "kcmpT", [128, 256], BF16)
        vcmp = sb(nc, es, tag + "vcmp", [128, 2, 128], BF16)
        rb = sb(nc, es, tag + "rb", [128, 4, 32], F32)
        rbd = sb(nc, es, tag + "rbd", [128, 4, 32], F32)
        bkc = sb(nc, es, tag + "bkc", [128, 16], F32)
        bk0 = sb(nc, es, tag + "bk0", [128, 128], F32)
        bk1 = sb(nc, es, tag + "bk1", [128, 128], F32)
        Bc = sb(nc, es, tag + "Bc", [128, 4, 16], F32)
        Bf = sb(nc, es, tag + "Bf", [128, 2, 4, 128], F32)
        Bt1 = sb(nc, es, tag + "Bt1", [128, 128], F32)
        Bt2 = sb(nc, es, tag + "Bt2", [128, 128], F32)
        Bhi = sb(nc, es, tag + "Bhi", [128, 2, 4, 128], BF16)
        Blo = sb(nc, es, tag + "Blo", [128, 2, 4, 128], BF16)
        wm4 = sb(nc, es, tag + "wm4", [128, 128], BF16)
        E_all = sb(nc, es, tag + "E", [64, 32, 128], BF16)
        ovl = sb(nc, es, tag + "ovl", [128, 2, 64], BF16)
        selC = Ring([sb(nc, es, tag + "selC%d" % i, [128, 2, 64], F32) for i in range(2)])
        s_sel = [P.slot_sem() for _ in range(2)]
        pbuf = Ring([sb(nc, es, tag + "p%d" % i, [128, 256], F32) for i in range(2)])
        pn = Ring([sb(nc, es, tag + "pn%d" % i, [128, 256], BF16) for i in range(2)])
        pT = Ring([sb(nc, es, tag + "pT%d" % i, [128, 2, 128], BF16) for i in range(2)])
        rs = sb(nc, es, tag + "rs", [128, 16], F32)
        sc = sb(nc, es, tag + "sc", [128, 2, 64], F32)
        m8 = sb(nc, es, tag + "m8", [128, 3, 8], F32)
        negsel = sb(nc, es, tag + "negsel", [128, 64], BF16)
        negselT = Ring([sb(nc, es, tag + "nsT%d" % i, [64, 128], BF16) for i in range(2)])
        PT = Ring([sb(nc, es, tag + "PT%d" % i, [128, 128], BF16) for i in range(4)])
        acc = Ring([sb(nc, es, tag + "acc%d" % i, [128, 512], F32) for i in range(2)])
        accb = Ring([sb(nc, es, tag + "accb%d" % i, [128, 512], BF16) for i in range(2)])
        coef = sb(nc, es, tag + "coef", [128, 16], F32)
        ostage = Ring([sb(nc, es, tag + "os%d" % i, [128, 4, 512], BF16) for i in range(2)])
        s_os = [P.slot_sem() for _ in range(2)]
        banks = Ring([ps(nc, es, tag + "ps%d" % i, [128, 512]) for i in range(5)])
        obanks = Ring([ps(nc, es, tag + "pso%d" % i, [128, 512]) for i in range(2)])
        impbank = ps(nc, es, tag + "psimp", [128, 512])
        imp_free = [None]

        wst = Ring([sb(nc, es, tag + "wst%d" % i, [128, 1, NW], F32) for i in range(1)])
        s_wst = [P.slot_sem() for _ in range(1)]
        wv = M["w_nsa"].rearrange("(c p) n -> p c n", p=128)
        wcast = []
        for c2 in range(16):
            kws, wsb, fws = wst.next()
            tl = P.dma("sp", wsb[:, :, :], wv[:, c2:c2 + 1, :], s_wst[kws], waits=[fws])
            tcw = P.op("dve", lambda e, wsb=wsb, c2=c2: e.tensor_copy(out=w[:, c2:c2 + 1, :], in_=wsb[:, :, :]), waits=[tl])
            wst.release(kws, tcw)
            wcast.append(tcw)
        t_w = wcast
        w1cast = []
        for i, nm in enumerate(("k", "v")):
            for hf in range(4):
                kws, wsb, fws = wst.next()
                flat = wsb[:, :, :].rearrange("p a n -> p (a n)")
                tl = P.dma("sp", flat[:, 0:1024].rearrange("p (l f) -> p l f", l=8), M["w1_" + nm][:, hf * 8:(hf + 1) * 8, :],
                           s_wst[kws], waits=[fws])
                tcw = P.op("dve", lambda e, flat=flat, i=i, hf=hf: e.tensor_copy(
                    out=w1[i][:, hf * 8:(hf + 1) * 8, :], in_=flat[:, 0:1024].rearrange("p (l f) -> p l f", l=8)), waits=[tl])
                wst.release(kws, tcw)
                w1cast.append(tcw)
            kws, wsb, fws = wst.next()
            flat = wsb[:, :, :].rearrange("p a n -> p (a n)")
            P.dma("sp", flat[:, 0:128], M["w2_" + nm], s_wst[kws], waits=[fws])
            tl = P.dma("sp", flat[:, 128:160], M["peT_" + nm], s_wst[kws])
            tcw = P.op("dve", lambda e, flat=flat, i=i: e.tensor_copy(out=w2[i][:, :], in_=flat[:, 0:128]), waits=[tl])
            tcw = P.op("dve", lambda e, flat=flat, i=i: e.tensor_copy(out=peT[i][:, :], in_=flat[:, 128:160]), waits=[tl])
            wst.release(kws, tcw)
            w1cast.append(tcw)
        t_w1 = w1cast
        s_c = P.slot_sem()
        P.dma("sp", ident[:, :], C["ident"], s_c)
        P.dma("sp", rb[:, :, :], M["rb_bc"], s_c)
        P.dma("sp", bkc[:, :], M["bk_c"], s_c)
        P.dma("sp", bk0[:, :], M["bk_0"], s_c)
        P.dma("sp", bk1[:, :], M["bk_1"], s_c)
        P.dma("sp", wm4[:, :], M["wm4"], s_c)
        P.dma("sp", E_all[:, :, :], M["E_all"], s_c)
        t_c = P.dma("sp", ovl[:, :, :], M["ovl"], s_c)
        t_ones = P.op("pool", lambda e: e.memset(vaug[:, :, :, 128:130], 1.0))
        t_z = None
        for b_ in (pn.bufs[0], pn.bufs[1], hdn[0], hdn[1], kcmpT):
            t_z = P.op("pool", lambda e, b_=b_: e.memset(b_[:, :], 0.0))
        t_z = P.op("pool", lambda e: e.memset(vcmp[:, :, :], 0.0))
        tb = None
        for h in range(4):
            tb = P.op("pool", lambda e, h=h: e.tensor_scalar(out=rbd[:, h, :], in0=rb[:, h, :], scalar1=rb[:, h, 31:32],
                                                              scalar2=None, op0=ALU.subtract), waits=[t_c])
        t_bias = None
        for h in range(4):
            for (bk, dst, hasmask) in ((bkc[:, :], Bc[:, h, :], True), (bk0[:, :], Bf[:, 0, h, :], True),
                                       (bk1[:, :], Bf[:, 1, h, :], False)):
                n_ = 16 if bk is not None and dst.shape[-1] == 16 else 128
                t1v = Bt1[:, 0:n_]
                tb = P.op("pool", lambda e, bk=bk, dst=dst: e.tensor_scalar(
                    out=dst, in0=bk, scalar1=-1.0, scalar2=NEGB, op0=ALU.is_equal, op1=ALU.mult), waits=[tb])
                for b in range(31):
                    tb = P.op("pool", lambda e, bk=bk, t1v=t1v, b=b, h=h: e.tensor_scalar(
                        out=t1v, in0=bk, scalar1=float(b), scalar2=rbd[:, h, b:b + 1], op0=ALU.is_equal, op1=ALU.mult),
                        waits=[tb])
                    tb = P.op("pool", lambda e, dst=dst, t1v=t1v: e.tensor_tensor(out=dst, in0=dst, in1=t1v, op=ALU.add),
                              waits=[tb])
            for dl in range(2):
                tb = P.op("pool", lambda e, dl=dl, h=h: e.tensor_copy(out=Bhi[:, dl, h, :], in_=Bf[:, dl, h, :]), waits=[tb])
                tb = P.op("pool", lambda e, dl=dl, h=h: e.tensor_copy(out=Bt2[:, :], in_=Bhi[:, dl, h, :]), waits=[tb])
                tb = P.op("pool", lambda e, dl=dl, h=h: e.tensor_tensor(out=Bt2[:, :], in0=Bf[:, dl, h, :], in1=Bt2[:, :],
                                                                          op=ALU.subtract), waits=[tb])
                tb = P.op("pool", lambda e, dl=dl, h=h: e.tensor_copy(out=Blo[:, dl, h, :], in_=Bt2[:, :]), waits=[tb])
        t_bias = tb

        hv = hT_all_t.ap()
        NBLK = SEQ // BT

        def load_blk(b):
            T0 = b * BT
            r, col0 = T0 // NTOK, T0 % NTOK
            k, buf, fr = hc.next()
            t1 = load_hT_block(P, "sp", buf, hv, r, col0, BT, s_hc[k], [fr])
            return k, buf, t1
        nxt = load_blk(0)
        proj_done = []
        for b in range(NBLK):
            k, hcb, t_h = nxt
            if b + 1 < NBLK:
                nxt = load_blk(b + 1)
            T0 = b * BT
            last = None
            for gi in range(8):
                col0 = gi * 128
                kb, bank, fb = banks.next()
                for c in range(NCH):
                    tm = P.op("pe", lambda e, c=c, bank=bank, col0=col0, hcb=hcb: e.matmul(
                        bank[:, 0:BT], lhsT=w[:, c, col0:col0 + 128], rhs=hcb[:, c, :],
                        start=(c == 0), stop=(c == NCH - 1)), waits=[t_w, t_h, fb] if c == 0 else (), sig=(c == NCH - 1))
                if gi < 4:
                    te = P.op("act", lambda e, bank=bank, gi=gi, T0=T0: e.mul(out=qT[:, gi, T0:T0 + BT], in_=bank[:, 0:BT], mul=SCALE),
                              waits=[tm])
                else:
                    te = P.op("dve", lambda e, bank=bank, gi=gi, T0=T0: e.tensor_copy(out=kvT[:, gi - 4, T0:T0 + BT], in_=bank[:, 0:BT]),
                              waits=[tm])
                banks.release(kb, te)
                proj_done.append(te)
            for t in range(BT // 128):
                tile_i = (T0 // 128) + t
                kb, bank, fb = banks.next()
                for c in range(NCH):
                    tm = P.op("pe", lambda e, c=c, bank=bank, t=t, hcb=hcb: e.matmul(
                        bank[:, 0:268], lhsT=hcb[:, c, t * 128:(t + 1) * 128], rhs=w[:, c, O_VS:NW],
                        start=(c == 0), stop=(c == NCH - 1)), waits=[fb] if c == 0 else (), sig=(c == NCH - 1))
                last = tm
                ta = P.op("dve", lambda e, bank=bank, tile_i=tile_i: e.tensor_copy(
                    out=vaug[:, :, tile_i, 0:128], in_=bank[:, 0:256].rearrange("p (a d) -> p a d", a=2)), waits=[tm])
                tg = P.op("act", lambda e, bank=bank, tile_i=tile_i: e.activation(
                    out=sig[:, tile_i, :], in_=bank[:, 256:268], func=AF.Sigmoid), waits=[tm])
                banks.release(kb, [ta, tg])
                proj_done += [ta, tg]
            hc.release(k, last)

        cmp_done = []
        for i in range(2):
            src = kvT[:, i, :]
            kb, bank, fb = banks.next()
            for l in range(32):
                tm = P.op("pe", lambda e, l=l, bank=bank, i=i: e.matmul(
                    bank[:, 0:1], lhsT=w1[i][:, l, :], rhs=peT[i][:, l:l + 1], start=(l == 0), stop=(l == 31)),
                    waits=[fb, t_w1] if l == 0 else (), sig=(l == 31))
            tcv = P.op("dve", lambda e, bank=bank, i=i: e.tensor_copy(out=cvec[:, i:i + 1], in_=bank[:, 0:1]), waits=[tm])
            banks.release(kb, tcv)
            kb, bank, fb = banks.next()
            for l in range(32):
                tm = P.op("pe", lambda e, l=l, bank=bank, i=i, src=src: e.matmul(
                    bank[:, 0:255], lhsT=w1[i][:, l, :], rhs=src[:, l:l + 16 * 254 + 1:16], start=(l == 0), stop=(l == 31)),
                    waits=[fb, proj_done] if l == 0 else (), sig=(l == 31))
            th = P.op("act", lambda e, bank=bank, i=i: e.activation(out=hdn[i][:, 0:255], in_=bank[:, 0:255], func=AF.Silu,
                                                                    bias=cvec[:, i:i + 1]), waits=[tm, tcv, t_z])
            banks.release(kb, th)
            if i == 0:
                kb, bank, fb = banks.next()
                tm = P.op("pe", lambda e, bank=bank: e.matmul(bank[:, 0:255], lhsT=w2[0][:, :], rhs=hdn[0][:, 0:255],
                                                              start=True, stop=True), waits=[fb, th])
                tk = P.op("dve", lambda e, bank=bank: e.tensor_copy(out=kcmpT[:, 0:255], in_=bank[:, 0:255]), waits=[tm, t_z])
                banks.release(kb, tk)
                cmp_done.append(tk)
            else:
                for nt, cnt in ((0, 128), (1, 127)):
                    kb, bank, fb = banks.next()
                    tm = P.op("pe", lambda e, bank=bank, nt=nt, cnt=cnt: e.matmul(
                        bank[0:cnt, 0:128], lhsT=hdn[1][:, nt * 128:nt * 128 + cnt], rhs=w2[1][:, :], start=True, stop=True),
                        waits=[fb, th])
                    tk = P.op("dve", lambda e, bank=bank, nt=nt, cnt=cnt: e.tensor_copy(out=vcmp[0:cnt, nt, :], in_=bank[0:cnt, 0:128]),
                              waits=[tm, t_z])
                    banks.release(kb, tk)
                    cmp_done.append(tk)

        yv = yT_in_t.ap()
        ready = [proj_done, cmp_done, t_bias, t_ones]
        kos = osb = fos = None
        os_done = []
        for qt in range(nqt):
            q0 = qt * 128
            qs = slice(q0, q0 + 128)
            ksl, slb, fsl = selC.next()
            P.dma("sp", slb[:, 0, :], M["selC"][qt], s_sel[ksl], waits=[fsl])
            t_sl = P.dma("sp", slb[:, 1, :], M["selF"][qt], s_sel[ksl])
            kac, accq, fac = acc.next()
            W = min(255, q0 // 16 + 7)
            nlo = max(0, q0 // 16 - 9)
            c0 = nlo - q0 // 16 + 9
            nts = 2 if W > 128 else 1
            bimp, fbi = impbank, imp_free[0]
            t_imp = None
            acc_tok = [None] * 4
            for h in range(4):
                kb, bL, fb = banks.next()
                tm = P.op("pe", lambda e, bL=bL, h=h, qs=qs, W=W: e.matmul(
                    bL[:, 0:W], lhsT=qT[:, h, qs], rhs=kcmpT[:, 0:W], start=True, stop=True), waits=[fb, ready])
                tbd = P.op("dve", lambda e, bL=bL, h=h, nlo=nlo, W=W, c0=c0: e.tensor_tensor(
                    out=bL[:, nlo:W], in0=bL[:, nlo:W], in1=Bc[:, h, c0:c0 + (W - nlo)], op=ALU.add), waits=[tm, t_bias])
                kp, pb, fp = pbuf.next()
                col = (qt % 2) * 8 + h
                te = P.op("act", lambda e, bL=bL, pb=pb, W=W, col=col: e.activation(
                    out=pb[:, 0:W], in_=bL[:, 0:W], func=AF.Exp, accum_out=rs[:, col:col + 1]), waits=[tbd, fp])
                banks.release(kb, te)
                t1_ = P.op("dve", lambda e, col=col: e.tensor_scalar(out=rs[:, col:col + 1], in0=rs[:, col:col + 1],
                                                                     scalar1=1e-30, scalar2=None, op0=ALU.max), waits=[te])
                t2_ = P.op("dve", lambda e, col=col: e.reciprocal(out=rs[:, col:col + 1], in_=rs[:, col:col + 1]), waits=[t1_])
                kn, pnb, fn_ = pn.next()
                t3_ = P.op("dve", lambda e, pb=pb, pnb=pnb, W=W, col=col: e.tensor_scalar(
                    out=pnb[:, 0:W], in0=pb[:, 0:W], scalar1=rs[:, col:col + 1], scalar2=None, op0=ALU.mult),
                    waits=[t2_, fn_, t_z])
                pbuf.release(kp, t3_)
                kb, bT, fb = banks.next()
                bTb = bT[:, :].bitcast(BF16)
                for nt in range(nts):
                    tt_ = P.op("pe", lambda e, nt=nt, bTb=bTb, pnb=pnb: e.transpose(
                        out=bTb[:, nt * 128:(nt + 1) * 128], in_=pnb[:, nt * 128:(nt + 1) * 128], identity=ident[:, :]),
                        waits=[fb, t3_] if nt == 0 else (), sig=(nt == nts - 1))
                pn.release(kn, tt_)
                kpt, pTb, fpt = pT.next()
                tcp = P.op("act", lambda e, bTb=bTb, pTb=pTb, nts=nts: e.copy(
                    out=pTb[:, 0:nts, :], in_=bTb[:, 0:nts * 128].rearrange("p (a s) -> p a s", a=nts)), waits=[tt_, fpt])
                banks.release(kb, tcp)
                kb, bO, fb = obanks.next()
                for nt in range(nts):
                    to_ = P.op("pe", lambda e, nt=nt, bO=bO, pTb=pTb: e.matmul(
                        bO[:, 0:128], lhsT=pTb[:, nt, :], rhs=vcmp[:, nt, :], start=(nt == 0), stop=(nt == nts - 1)),
                        waits=[fb, tcp] if nt == 0 else (), sig=(nt == nts - 1))
                for nt in range(nts):
                    t_imp = P.op("pe", lambda e, nt=nt, bimp=bimp, pTb=pTb, h=h: e.matmul(
                        bimp[:, 0:64], lhsT=pTb[:, nt, :], rhs=ovl[:, nt, :],
                        start=(h == 0 and nt == 0), stop=(h == 3 and nt == nts - 1)),
                        waits=[fbi] if (h == 0 and nt == 0) else (), sig=(nt == nts - 1))
                pT.release(kpt, t_imp)
                ta_ = P.op("dve", lambda e, bO=bO, accq=accq, h=h, qt=qt: e.tensor_scalar(
                    out=accq[:, h * 128:(h + 1) * 128], in0=bO[:, 0:128], scalar1=sig[:, qt, h * 3:h * 3 + 1],
                    scalar2=None, op0=ALU.mult), waits=[to_, fac])
                obanks.release(kb, ta_)
                acc_tok[h] = ta_
            ts1 = P.op("dve", lambda e, bimp=bimp, slb=slb: e.tensor_tensor(out=sc[:, 0, :], in0=bimp[:, 0:64], in1=slb[:, 0, :],
                                                                            op=ALU.mult), waits=[t_imp, t_sl])
            imp_free[0] = ts1
            ts2 = P.op("dve", lambda e, slb=slb: e.tensor_tensor(out=sc[:, 0, :], in0=sc[:, 0, :], in1=slb[:, 1, :], op=ALU.add),
                       waits=[ts1])
            selC.release(ksl, ts2)
            ts3 = P.op("dve", lambda e: e.max(out=m8[:, 0, :], in_=sc[:, 0, :]), waits=[ts2])
            ts4 = P.op("dve", lambda e: e.match_replace(out=sc[:, 1, :], in_to_replace=m8[:, 0, :], in_values=sc[:, 0, :],
                                                        imm_value=NEGB), waits=[ts3])
            ts5 = P.op("dve", lambda e: e.max(out=m8[:, 1, :], in_=sc[:, 1, :]), waits=[ts4])
            ts6 = P.op("dve", lambda e: e.tensor_scalar(out=m8[:, 2, 0:1], in0=m8[:, 1, 7:8], scalar1=-5e29, scalar2=None,
                                                        op0=ALU.max), waits=[ts5])
            ts7 = P.op("dve", lambda e: e.tensor_scalar(out=negsel[:, :], in0=sc[:, 0, :], scalar1=m8[:, 2, 0:1], scalar2=-30000.0,
                                                        op0=ALU.is_lt, op1=ALU.mult), waits=[ts6, getattr(P, "negsel_free", None)])
            kb, bT, fb = banks.next()
            bTb = bT[:, :].bitcast(BF16)
            tt_ = P.op("pe", lambda e, bTb=bTb: e.transpose(out=bTb[0:64, 0:128], in_=negsel[:, 0:64], identity=ident[:, :]),
                       waits=[fb, ts7])
            P.negsel_free = tt_
            kns, nsT, fns = negselT.next()
            tns = P.op("act", lambda e, bTb=bTb, nsT=nsT: e.copy(out=nsT[:, :], in_=bTb[0:64, 0:128]), waits=[tt_, fns])
            banks.release(kb, tns)
            last_sel_pe = None
            for br in branches:
                kts = list(range(0, qt + 1)) if br == 0 else list(range(max(0, qt - 4), qt + 1))
                ksrc = 2 if br == 0 else 3
                for h in range(4):
                    kbo, bO, fbo = obanks.next()
                    pend = None

                    def finish(ii, kt, kb, bL, tm, bO=bO, fbo=fbo, br=br, kts=kts):
                        kpt, ptb, fpt = PT.next()
                        te = P.op("act", lambda e, bL=bL, ptb=ptb: e.activation(out=ptb[:, :], in_=bL[:, 0:128], func=AF.Exp),
                                  waits=[tm, fpt])
                        banks.release(kb, te)
                        to2 = P.op("pe", lambda e, bO=bO, ptb=ptb, br=br, kt=kt, ii=ii, kts=kts: e.matmul(
                            bO[:, 0:129], lhsT=ptb[:, :], rhs=vaug[:, br, kt, 0:129], start=(ii == 0), stop=(ii == len(kts) - 1)),
                            waits=[te, fbo] if ii == 0 else [te], sig=True)
                        PT.release(kpt, to2)
                        return to2
                    for ii, kt in enumerate(kts):
                        dl = qt - kt
                        ksl_ = slice(kt * 128, (kt + 1) * 128)
                        kb, bL, fb = banks.next()
                        extra = []
                        if br == 0:
                            extra.append((E_all[:, kt, :], nsT[:, :]))
                        if dl <= 1:
                            extra.append((ident[:, :], Bhi[:, dl, h, :]))
                            extra.append((ident[:, :], Blo[:, dl, h, :]))
                        if br == 1 and dl == 4:
                            extra.append((ident[:, :], wm4[:, :]))
                        tm = P.op("pe", lambda e, bL=bL, ksrc=ksrc, ksl_=ksl_, h=h, qs=qs, extra=extra: e.matmul(
                            bL[:, 0:128], lhsT=kvT[:, ksrc, ksl_], rhs=qT[:, h, qs], start=True, stop=(len(extra) == 0)),
                            waits=[fb, tns if br == 0 else None], sig=(len(extra) == 0))
                        for xi_, (l_, r_) in enumerate(extra):
                            tm = P.op("pe", lambda e, bL=bL, l_=l_, r_=r_, xi_=xi_, extra=extra: e.matmul(
                                bL[:, 0:128], lhsT=l_, rhs=r_, start=False, stop=(xi_ == len(extra) - 1)),
                                sig=(xi_ == len(extra) - 1))
                        if pend is not None:
                            to_ = finish(*pend)
                        pend = (ii, kt, kb, bL, tm)
                    to_ = finish(*pend)
                    last_sel_pe = to_
                    col = h * 2 + br
                    tc1 = P.op("dve", lambda e, bO=bO, col=col: e.reciprocal(out=coef[:, col:col + 1], in_=bO[:, 128:129]), waits=[to_])
                    tc2 = P.op("dve", lambda e, col=col, qt=qt, h=h, br=br: e.tensor_tensor(
                        out=coef[:, col:col + 1], in0=coef[:, col:col + 1], in1=sig[:, qt, h * 3 + 1 + br:h * 3 + 2 + br], op=ALU.mult),
                        waits=[tc1])
                    tc3 = P.op("dve", lambda e, bO=bO, accq=accq, h=h, col=col: e.scalar_tensor_tensor(
                        out=accq[:, h * 128:(h + 1) * 128], in0=bO[:, 0:128], scalar=coef[:, col:col + 1],
                        in1=accq[:, h * 128:(h + 1) * 128], op0=ALU.mult, op1=ALU.add), waits=[tc2, acc_tok[h]])
                    acc_tok[h] = tc3
                    obanks.release(kbo, tc3)
            negselT.release(kns, last_sel_pe)
            kab, ab, fab = accb.next()
            tcb = P.op("act", lambda e, ab=ab, accq=accq: e.copy(out=ab[:, :], in_=accq[:, :]), waits=[acc_tok, fab])
            acc.release(kac, tcb)
            kb, bT, fb = banks.next()
            bTb = bT[:, :].bitcast(BF16)
            for h in range(4):
                tt_ = P.op("pe", lambda e, h=h, bTb=bTb, ab=ab: e.transpose(
                    out=bTb[:, h * 128:(h + 1) * 128], in_=ab[:, h * 128:(h + 1) * 128], identity=ident[:, :]),
                    waits=[fb, tcb] if h == 0 else (), sig=(h == 3))
            accb.release(kab, tt_)
            if qt % 4 == 0:
                kos, osb, fos = ostage.next()
                os_done = []
            tcp = P.op("act", lambda e, bTb=bTb, osb=osb, qt=qt: e.copy(
                out=osb[:, :, (qt % 4) * 128:(qt % 4 + 1) * 128], in_=bTb[:, 0:512].rearrange("p (h s) -> p h s", h=4)),
                waits=[tt_, fos])
            banks.release(kb, tcp)
            os_done.append(tcp)
            if qt % 4 == 3:
                T0 = (qt - 3) * 128
                half, col0 = T0 // NTOK, T0 % NTOK
                t_st = None
                for h in range(4):
                    t_st = P.dma("sp", yv[half * 1024 + 512 + h * 128: half * 1024 + 512 + (h + 1) * 128, col0:col0 + 512],
                                 osb[:, h, :], s_os[kos], waits=[os_done] if h == 0 else ())
                ostage.release(kos, t_st)
        if do_ag:
            t_ag = coll_allgather(P, yT_in_t, yT_all_t, [ostage.free[0], ostage.free[1]])
            P.raw("pool", lambda e: None, waits=[t_ag])
        else:
            P.raw("sp", lambda e: None, waits=[ostage.free[0], ostage.free[1]])
        P.emit()
```
